# Optimizing a Trainium2 kernel written in Bass

```python
import math
import jax, jax.numpy as jnp
from jax import lax
import numpy as np

D_MODEL = 1024
BATCH = 32
SEQ = 2048
DEPTH = 1

ATTN_HEADS = 8
HEAD_DIM = 64
ATTN_WIDTH = ATTN_HEADS * HEAD_DIM
LRU_WIDTH = D_MODEL - ATTN_WIDTH
LRU_BLOCKS = 8
LRU_BLOCK_DIM = LRU_WIDTH // LRU_BLOCKS
MIX_WIDTH = ATTN_WIDTH + LRU_WIDTH
IN_WIDTH = 4 * ATTN_WIDTH + 2 * LRU_WIDTH
MOBA_BLOCK = 256
MOBA_TOPK = 3
QUERY_CHUNK = 8
CONV_WIDTH = 4
LRU_C = 8.0
ROPE_THETA = 10000.0
EPS = 1e-6

kernel_name = "hymba_moba_rglru_block"


def rmsnorm(x, gain):
    xf = x.astype(jnp.float32)
    y = xf * lax.rsqrt(jnp.mean(xf * xf, axis=-1, keepdims=True) + EPS)
    return (y * gain.astype(jnp.float32)).astype(x.dtype)


def rope_tables(seq_len):
    pos = jnp.arange(seq_len, dtype=jnp.float32)
    inv_freq = ROPE_THETA ** (-jnp.arange(0, HEAD_DIM, 2, dtype=jnp.float32) / HEAD_DIM)
    ang = pos[:, None] * inv_freq[None, :]
    return jnp.cos(ang), jnp.sin(ang)


def apply_rope(t, cos, sin):
    tf = t.astype(jnp.float32)
    t1, t2 = tf[..., : HEAD_DIM // 2], tf[..., HEAD_DIM // 2:]
    c, s = cos[None, :, None, :], sin[None, :, None, :]
    return jnp.concatenate([t1 * c - t2 * s, t2 * c + t1 * s], axis=-1).astype(t.dtype)


def moba_attention(q, k, v):
    B, S, H, Dh = q.shape
    n_blk = -(-S // MOBA_BLOCK)
    s_pad = n_blk * MOBA_BLOCK
    pad = ((0, 0), (0, s_pad - S), (0, 0), (0, 0))
    q, k, v = [jnp.pad(t, pad).transpose(0, 2, 1, 3) for t in (q, k, v)]
    k_blk = k.reshape(B, H, n_blk, MOBA_BLOCK, Dh)
    v_blk = v.reshape(B, H, n_blk, MOBA_BLOCK, Dh)
    k_mean = jnp.mean(k_blk.astype(jnp.float32), axis=3).astype(q.dtype)

    q_blk_id = jnp.arange(s_pad) // MOBA_BLOCK
    gate = jnp.einsum('bhsd,bhnd->bhsn', q, k_mean).astype(jnp.float32)
    fully_past = jnp.arange(n_blk)[None, :] < q_blk_id[:, None]
    gate = jnp.where(fully_past, gate, -jnp.inf)
    k_eff = min(MOBA_TOPK, n_blk)
    _, sel_idx = lax.top_k(gate, k_eff)
    n_valid = jnp.minimum(q_blk_id, k_eff)
    sel_valid = jnp.arange(k_eff)[None, :] < n_valid[:, None]

    scale = Dh ** -0.5
    gather_blocks = jax.vmap(jax.vmap(lambda blocks, idx: blocks[idx]))

    def attend_chunk(ci):
        start = ci * QUERY_CHUNK
        q_c = lax.dynamic_slice_in_dim(q, start, QUERY_CHUNK, axis=2)
        idx_c = lax.dynamic_slice_in_dim(sel_idx, start, QUERY_CHUNK, axis=2)
        valid_c = lax.dynamic_slice_in_dim(sel_valid, start, QUERY_CHUNK, axis=0)
        blk_start = (start // MOBA_BLOCK) * MOBA_BLOCK
        k_own = lax.dynamic_slice_in_dim(k, blk_start, MOBA_BLOCK, axis=2)
        v_own = lax.dynamic_slice_in_dim(v, blk_start, MOBA_BLOCK, axis=2)
        qpos = start + jnp.arange(QUERY_CHUNK)
        kpos = blk_start + jnp.arange(MOBA_BLOCK)
        s_own = jnp.einsum('bhqd,bhkd->bhqk', q_c, k_own).astype(jnp.float32) * scale
        s_own = jnp.where(kpos[None, :] <= qpos[:, None], s_own, -jnp.inf)
        k_g = gather_blocks(k_blk, idx_c)
        v_g = gather_blocks(v_blk, idx_c)
        s_sel = jnp.einsum('bhqd,bhqjkd->bhqjk', q_c, k_g).astype(jnp.float32) * scale
        s_sel = jnp.where(valid_c[:, :, None], s_sel, -jnp.inf)
        s_sel = s_sel.reshape(B, H, QUERY_CHUNK, k_eff * MOBA_BLOCK)
        p = jax.nn.softmax(jnp.concatenate([s_own, s_sel], axis=-1), axis=-1).astype(v.dtype)
        p_own = p[..., :MOBA_BLOCK]
        p_sel = p[..., MOBA_BLOCK:].reshape(B, H, QUERY_CHUNK, k_eff, MOBA_BLOCK)
        return (jnp.einsum('bhqk,bhkd->bhqd', p_own, v_own)
                + jnp.einsum('bhqjk,bhqjkd->bhqd', p_sel, v_g))

    n_chunks = s_pad // QUERY_CHUNK
    out = lax.map(attend_chunk, jnp.arange(n_chunks))
    out = out.transpose(1, 0, 3, 2, 4).reshape(B, s_pad, H, Dh)
    return out[:, :S]


def causal_depthwise_conv(x, w, b):
    out = lax.conv_general_dilated(
        x, w[:, None, :], window_strides=(1,), padding=[(CONV_WIDTH - 1, 0)],
        dimension_numbers=('NWC', 'WIO', 'NWC'), feature_group_count=x.shape[-1])
    return out + b


def rg_lru(x, w_r, b_r, w_i, b_i, lam):
    B, S, W = x.shape
    xb = x.reshape(B, S, LRU_BLOCKS, LRU_BLOCK_DIM)
    r = jax.nn.sigmoid((jnp.einsum('bsgi,gij->bsgj', xb, w_r) + b_r).astype(jnp.float32)).reshape(B, S, W)
    i = jax.nn.sigmoid((jnp.einsum('bsgi,gij->bsgj', xb, w_i) + b_i).astype(jnp.float32)).reshape(B, S, W)
    log_a = -LRU_C * r * jax.nn.softplus(-lam.astype(jnp.float32))
    a = jnp.exp(log_a)
    norm = jnp.sqrt(-jnp.expm1(2.0 * log_a))
    u = norm * i * x.astype(jnp.float32)

    def combine(left, right):
        a1, b1 = left
        a2, b2 = right
        return a1 * a2, a2 * b1 + b2

    _, h = lax.associative_scan(combine, (a, u), axis=1)
    return h.astype(x.dtype)


def setup_inputs(seed: int = 0) -> dict:
    key = jax.random.key(seed)
    ks = jax.random.split(key, 20)
    f32 = jnp.float32
    nrm = lambda k, shape, s: jax.random.normal(k, shape, f32) * s
    u = jax.random.uniform(ks[13], (DEPTH, LRU_WIDTH), f32, 0.9, 0.999)
    a0 = u ** (1.0 / LRU_C)
    lru_lambda = jnp.log(a0) - jnp.log1p(-a0)
    return {
        "x": nrm(ks[0], (BATCH, SEQ, D_MODEL), 1.0),
        "c": nrm(ks[1], (BATCH, D_MODEL), 1.0),
        "w_mod": nrm(ks[2], (DEPTH, D_MODEL, 3 * D_MODEL), 0.5 * D_MODEL ** -0.5),
        "b_mod": nrm(ks[3], (DEPTH, 3 * D_MODEL), 0.01),
        "norm_gain": 1.0 + nrm(ks[4], (DEPTH, D_MODEL), 0.02),
        "w_in": nrm(ks[5], (DEPTH, D_MODEL, IN_WIDTH), D_MODEL ** -0.5),
        "conv_w": nrm(ks[6], (DEPTH, CONV_WIDTH, LRU_WIDTH), CONV_WIDTH ** -0.5),
        "conv_b": nrm(ks[7], (DEPTH, LRU_WIDTH), 0.01),
        "w_rgate": nrm(ks[8], (DEPTH, LRU_BLOCKS, LRU_BLOCK_DIM, LRU_BLOCK_DIM), LRU_BLOCK_DIM ** -0.5),
        "b_rgate": nrm(ks[9], (DEPTH, LRU_BLOCKS, LRU_BLOCK_DIM), 0.01),
        "w_igate": nrm(ks[10], (DEPTH, LRU_BLOCKS, LRU_BLOCK_DIM, LRU_BLOCK_DIM), LRU_BLOCK_DIM ** -0.5),
        "b_igate": nrm(ks[11], (DEPTH, LRU_BLOCKS, LRU_BLOCK_DIM), 0.01),
        "lru_lambda": lru_lambda,
        "attn_out_gain": 1.0 + nrm(ks[14], (DEPTH, ATTN_WIDTH), 0.02),
        "lru_out_gain": 1.0 + nrm(ks[15], (DEPTH, LRU_WIDTH), 0.02),
        "w_out": nrm(ks[16], (DEPTH, MIX_WIDTH, D_MODEL), MIX_WIDTH ** -0.5),
        "final_gain": 1.0 + nrm(ks[17], (D_MODEL,), 0.02),
    }


def reference(x, c, w_mod, b_mod, norm_gain, w_in, conv_w, conv_b, w_rgate, b_rgate,
              w_igate, b_igate, lru_lambda, attn_out_gain, lru_out_gain, w_out, final_gain):
    B, S, _ = x.shape
    cos, sin = rope_tables(S)
    c_act = jax.nn.silu(c)
    splits = [ATTN_WIDTH, 2 * ATTN_WIDTH, 3 * ATTN_WIDTH, 4 * ATTN_WIDTH, 4 * ATTN_WIDTH + LRU_WIDTH]
    for l in range(DEPTH):
        mod = c_act @ w_mod[l] + b_mod[l]
        shift, scale, gate = jnp.split(mod, 3, axis=-1)
        h = rmsnorm(x, norm_gain[l]) * (1.0 + scale[:, None, :]) + shift[:, None, :]
        proj = h @ w_in[l]
        q, k, v, z_attn, x_lru, z_lru = jnp.split(proj, splits, axis=-1)

        q = apply_rope(q.reshape(B, S, ATTN_HEADS, HEAD_DIM), cos, sin)
        k = apply_rope(k.reshape(B, S, ATTN_HEADS, HEAD_DIM), cos, sin)
        v = v.reshape(B, S, ATTN_HEADS, HEAD_DIM)
        attn = moba_attention(q, k, v).reshape(B, S, ATTN_WIDTH)
        y_attn = rmsnorm(attn, attn_out_gain[l]) * jax.nn.silu(z_attn)

        xc = causal_depthwise_conv(x_lru, conv_w[l], conv_b[l])
        rec = rg_lru(xc, w_rgate[l], b_rgate[l], w_igate[l], b_igate[l], lru_lambda[l])
        y_lru = rmsnorm(rec, lru_out_gain[l]) * jax.nn.silu(z_lru)

        y = jnp.concatenate([y_attn, y_lru], axis=-1) @ w_out[l]
        x = x + gate[:, None, :] * y
    return rmsnorm(x, final_gain)
```

```python
import numpy as np
import concourse.bass as bass
import concourse.mybir as mybir
from concourse.bass_utils import run_bass_kernel_spmd

F32 = mybir.dt.float32
BF16 = mybir.dt.bfloat16
AF = mybir.ActivationFunctionType
ALU = mybir.AluOpType
AX = mybir.AxisListType

N_CORES = 8
D = 1024
S_FULL = 2048
NSEQ_FULL = 4
BLK = 256
NH = 8
HD = 64
EPS = 1e-6
NVEC = 64
PE_FILL = True
CP_W = 0.01


class Op:
    __slots__ = ("eng", "deps", "calls", "dur", "idx", "seq", "start", "end", "dma", "lat")


class Sched:
    ops = []
    on = False


class Eng:
    def __init__(self, nc, eng, name):
        self.nc = nc
        self.e = eng
        self.name = name
        self.sem = nc.alloc_semaphore("ms_" + name)
        self.n = 0
        self.seen = {}
        self.p_waits = []
        self.p_calls = []

    def wait(self, *toks):
        if Sched.on:
            self.p_waits.extend(_flat(toks))
            return
        for t in _flat(toks):
            self.raw_wait(_resolve(t))

    def raw_wait(self, t):
        sem, val, key = t
        if self.seen.get(key, 0) >= val:
            return
        self.e.wait_ge(sem, val)
        self.seen[key] = val

    def done(self, inst):
        if Sched.on:
            assert self.p_calls and self.p_calls[-1] is inst
            op = Op()
            op.eng, op.deps, op.calls = self, self.p_waits, self.p_calls
            op.dma = None
            op.dur = _est_dur(self.name, op.calls)
            op.lat = 0.0
            op.idx = len(Sched.ops)
            op.seq = None
            self.p_waits, self.p_calls = [], []
            Sched.ops.append(op)
            return op
        self.n += 1
        inst.then_inc(self.sem, 1)
        return (self.sem, self.n, self.name)


class LazyEng:
    def __init__(self, E):
        self._E = E

    def __getattr__(self, name):
        E = self._E

        def f(*a, **k):
            call = (name, a, k)
            E.p_calls.append(call)
            return call
        return f


def _resolve(t):
    if isinstance(t, Op):
        assert t.seq is not None, "dependency emitted after its consumer"
        if t.dma is not None:
            return (t.dma.sem, t.seq, t.dma.key)
        return (t.eng.sem, t.seq, t.eng.name)
    return t


def _ap_free(ap):
    n = 1
    for d in list(ap.shape)[1:]:
        n *= int(d)
    return n


def _est_dur(eng, calls):
    t = 0.0
    for (name, a, k) in calls:
        out = k.get("out", a[0] if a else None)
        n = _ap_free(out) if out is not None and hasattr(out, "shape") else 64
        if eng == "pe":
            if name == "transpose":
                t += 0.21
            else:
                lhs = k.get("lhsT", a[1] if len(a) > 1 else None)
                f32 = lhs is not None and lhs.dtype == F32
                rows = int(lhs.shape[0]) if lhs is not None else 128
                c = n / 2400.0
                if f32:
                    c *= 3.7
                elif rows <= 64:
                    c *= 0.56
                t += max(c, 0.035)
        elif eng == "act":
            t += 0.24 + n * 0.0005
        elif eng == "dve":
            t += 0.07 + n * 0.00105
        elif eng == "pool":
            t += 0.12 + n * 0.0021
        else:
            t += 0.1
    return t


def run_schedule(engs, pe_fill=None):
    ops = Sched.ops
    Sched.on = False
    per = {}
    for op in ops:
        per.setdefault(op.eng.name, []).append(op)
    tail = [0.0] * len(ops)
    for op in reversed(ops):
        t_ = tail[op.idx] + (op.lat if op.dma is not None else op.dur)
        tail[op.idx] = t_
        for d in op.deps:
            if isinstance(d, Op) and tail[d.idx] < t_ + 0.2:
                tail[d.idx] = t_ + 0.2 - 0.0
    pos = {e: 0 for e in per}
    free_t = {e: 0.0 for e in per}
    done_flag = [False] * len(ops)
    WIN = 64
    order = []
    remaining = len(ops)
    while remaining:
        best = None
        best_t = None
        best_key = None
        for e, lst in per.items():
            p = pos[e]
            while p < len(lst) and done_flag[lst[p].idx]:
                p += 1
            pos[e] = p
            win = 1 if e == "sp" else WIN
            cnt = 0
            q = p
            while q < len(lst) and cnt < win:
                op = lst[q]
                q += 1
                if done_flag[op.idx]:
                    continue
                cnt += 1
                ok = True
                t = free_t[e]
                for d in op.deps:
                    if isinstance(d, Op):
                        if not done_flag[d.idx]:
                            ok = False
                            break
                        te = d.end + (0.02 if d.eng is op.eng else 0.12)
                        if te > t:
                            t = te
                if not ok:
                    continue
                key = t - CP_W * tail[op.idx]
                if best is None or key < best_key - 1e-9 or (abs(key - best_key) <= 1e-9 and op.idx < best.idx):
                    best, best_t, best_key = op, t, key
        assert best is not None, "scheduler deadlock"
        best.start = best_t
        if best.dma is not None:
            best.end = best_t + best.lat
            free_t[best.eng.name] = best_t + 0.08
        else:
            best.end = best_t + best.dur
            free_t[best.eng.name] = best.end
        done_flag[best.idx] = True
        order.append(best)
        remaining -= 1
    order.sort(key=lambda o: (o.start, o.idx))
    pe_free = 0.0
    n_fill = 0
    for op in order:
        E = op.eng
        if E.name == "pe":
            gap = op.start - pe_free
            if pe_fill is not None and gap > 0.3:
                for _ in range(min(int((gap - 0.08) / 0.215), 80)):
                    pe_fill()
                    n_fill += 1
            pe_free = op.end
        for d in op.deps:
            E.raw_wait(_resolve(d))
        inst = None
        for (name, a, k) in op.calls:
            inst = getattr(E.e, name)(*a, **k)
        if op.dma is not None:
            inst.then_inc(op.dma.sem, 16)
        else:
            E.n += 1
            inst.then_inc(E.sem, 1)
            op.seq = E.n
    span = max(o.end for o in order) if order else 0.0
    Sched.ops = []
    return span


def _flat(toks):
    out = []
    for t in toks:
        if t is None:
            continue
        if isinstance(t, Op):
            out.append(t)
        elif isinstance(t, (list, tuple)) and len(t) == 3 and isinstance(t[2], str):
            out.append(t)
        elif isinstance(t, (list, tuple)):
            out.extend(_flat(t))
        else:
            raise TypeError(t)
    return out


class DmaSem:
    _cnt = 0

    def __init__(self, nc, name):
        self.sem = nc.alloc_semaphore("dma_" + name)
        self.n = 0
        DmaSem._cnt += 1
        self.key = "dma_%s_%d" % (name, DmaSem._cnt)

    def issue(self, inst, eng=None, nbytes=0):
        self.n += 16
        if Sched.on:
            assert eng.p_calls and eng.p_calls[-1] is inst
            op = Op()
            op.eng, op.deps, op.calls = eng, eng.p_waits, eng.p_calls
            op.dma = self
            op.seq = self.n
            op.dur = 0.08
            op.lat = 2.2 + nbytes / 150e3
            op.idx = len(Sched.ops)
            eng.p_waits, eng.p_calls = [], []
            Sched.ops.append(op)
            return op
        inst.then_inc(self.sem, 16)
        return (self.sem, self.n, self.key)


class Banks:
    def __init__(self, nc):
        self.t = [nc.alloc_psum_tensor("bank%d" % i, [128, 512], F32) for i in range(8)]
        self.free = list(range(8))
        self.rel = {i: None for i in range(8)}

    def get(self):
        assert self.free, "out of PSUM banks"
        i = self.free.pop(0)
        return i, self.t[i], self.rel[i]

    def put(self, i, *toks):
        self.rel[i] = _flat(toks)
        self.free.append(i)


def build_program(NSEQ=NSEQ_FULL, S=S_FULL, stop_after=None):
    NB = S // BLK
    NT = S // 128
    nc = bass.Bass("TRN2", target_bir_lowering=False)

    def din(name, shape, dt=F32):
        return nc.dram_tensor(name, list(shape), dt, kind="ExternalInput").ap()

    x_d = din("x", [NSEQ, S, D])
    cT_d = din("cT", [128, 8, NSEQ])
    wmod_d = din("w_mod", [D, 3 * D])
    bgate_d = din("bgate", [1, D])
    win_d = din("w_in", [D, 3 * D])
    wout_d = din("w_out", [D, D])
    wr_d = din("w_r", [8, 64, 64])
    wi_d = din("w_i", [8, 64, 64])
    vec_d = din("vecs", [128, NVEC])
    fg_d = din("fgain", [1, D])
    cos_d = din("cosT", [128, S])
    sin_d = din("sinS", [128, S])
    id_d = din("ident", [128, 128])
    rm_d = din("rmat", [128, 128])
    tri_d = din("tri", [128, 128])
    sel4_d = din("sel4", [NSEQ, NSEQ * 128])
    out_d = nc.dram_tensor("out", [NSEQ, S, D], F32, kind="ExternalOutput").ap()

    PE = Eng(nc, nc.tensor, "pe")
    ACT = Eng(nc, nc.scalar, "act")
    DVE = Eng(nc, nc.vector, "dve")
    POOL = Eng(nc, nc.gpsimd, "pool")
    SP = Eng(nc, nc.sync, "sp")
    banks = Banks(nc)

    def sb(name, shape, dt=F32):
        return nc.alloc_sbuf_tensor("s_" + name, list(shape), dt)

    win_sb = sb("win_sb", [128, 8, 3 * D], BF16)
    wout_sb = sb("wout_sb", [128, 8, D], BF16)
    kT = sb("kT", [128, 4, S], BF16)
    vaug = sb("vaug", [128, NT, NH, HD + 1], BF16)
    ident = sb("ident", [128, 128])
    rmat = sb("rmat", [128, 128])
    onesm = sb("onesm", [128, 128])
    tri = sb("tri", [128, 128], BF16)
    tri32 = sb("tri32", [128, 128])
    wr_bd = sb("wr_bd", [128, 4, 128], BF16)
    wi_bd = sb("wi_bd", [128, 4, 128], BF16)
    vecs = sb("vecs", [128, NVEC])
    gsT = sb("gsT", [128, 8, NSEQ])
    shT = sb("shT", [128, 8, NSEQ])
    coef = sb("coef", [128, 4])
    coef2 = sb("coef2", [128, 4])
    nbr = sb("nbr", [128, 4])
    nbi = sb("nbi", [128, 4])
    gate_rows = sb("gate_rows", [NSEQ, D])
    sel4 = sb("sel4", [NSEQ, NSEQ, 128])
    gate_bc = sb("gate_bc", [128, D])
    fgain_bc = sb("fgain_bc", [128, D])
    kmean = sb("kmean", [128, 4, 8], BF16)
    hcarry = sb("hcarry", [128, 4])
    NXB = 2
    xt = [sb("xt%d" % i, [128, D]) for i in range(NXB)]
    xr = [sb("xr%d" % i, [128, D]) for i in range(2)]
    junk = sb("junk", [128, D], BF16)
    stat = sb("stat", [128, 16])
    hT = sb("hT", [128, 8, BLK], BF16)
    kq32 = [sb("kq32_%d" % i, [128, BLK]) for i in range(2)]
    rt1 = [sb("rt1_%d" % i, [128, BLK]) for i in range(2)]
    rt2 = [sb("rt2_%d" % i, [128, BLK]) for i in range(2)]
    krot = sb("krot", [128, BLK])
    ksum = sb("ksum", [128, 4])
    qT = sb("qT", [128, 4, BLK], BF16)
    cosb = [sb("cosb0", [128, BLK])] * 2
    sinb = [sb("sinb0", [128, BLK])] * 2
    ez = [sb("ez%d" % i, [128, 512]) for i in range(2)]
    sza = sb("sza", [128, 2, 512])
    xl = sb("xl", [128, 4, BLK + 3])
    xc = [sb("xc%d" % i, [128, BLK]) for i in range(2)]
    xcb = [sb("xcb%d" % i, [128, BLK], BF16) for i in range(2)]
    eri = [sb("eri%d" % i, [128, 2, BLK]) for i in range(2)]
    er = [eri[i][:, 0, :] for i in range(2)]
    ei = [eri[i][:, 1, :] for i in range(2)]
    aa = [sb("aa%d" % i, [128, BLK]) for i in range(2)]
    a2 = [sb("a2_%d" % i, [128, BLK]) for i in range(2)]
    uu = [sb("uu%d" % i, [128, BLK]) for i in range(2)]
    rec = [sb("rec%d" % i, [128, BLK]) for i in range(2)]
    sqb = [sb("sqb%d" % i, [128, BLK]) for i in range(2)]
    rstd_bc = sb("rstd_bc", [128, BLK])
    ylT2 = [sb("ylT%d" % i, [128, 4, BLK], BF16) for i in range(2)]
    NPT = 4
    PT = [sb("PT%d" % i, [128, 2, BLK], BF16) for i in range(NPT)]
    accb = sb("accb", [128, 2 * NH * (HD + 1)])
    acc = accb[:, :].rearrange("p (t h d) -> p t h d", t=2, h=NH)
    szl = sb("szl", [128, 4, BLK])
    ctmp = [sb("ctmp%d" % i, [128, 2, 2, HD + 1]) for i in range(2)]
    rden = sb("rden", [128, 2, NH])
    ypat = sb("ypat", [128, 1024])
    attn = ypat[:, :].rearrange("p (t f) -> p t f", t=2)
    yp = ypat[:, :].rearrange("p (c t) -> p c t", c=4)
    yaT = sb("yaT", [128, 4, BLK], BF16)
    g32 = sb("g32", [128, 2, NH, 8])
    cmpb = sb("cmpb", [128, NH, 8, 8])
    rank = sb("rank", [128, NH, 8])
    selx = sb("selx", [128, 2, NH, 8])
    otmp = ez

    V_NG, V_BSH, V_BSC, V_CW, V_CB, V_LAM, V_BR, V_BI, V_GL, V_GA = 0, 8, 16, 24, 40, 44, 48, 52, 56, 60

    stg = [xt[0], xt[1], xr[0], xr[1], ypat]
    ld = DmaSem(nc, "setup")
    t_small = []
    for (dst, src) in ((ident, id_d), (rmat, rm_d), (tri32, tri_d), (vecs, vec_d)):
        t_small.append(ld.issue(nc.sync.dma_start(out=dst[:], in_=src)))
    cTs = sb("cTs", [128, 8, NSEQ])
    t_small.append(ld.issue(nc.sync.dma_start(out=cTs[:], in_=cT_d)))
    t_small.append(ld.issue(nc.sync.dma_start(out=fgain_bc[:], in_=fg_d.to_broadcast((128, D)))))
    t_small.append(ld.issue(nc.sync.dma_start(out=gate_rows[:], in_=bgate_d.to_broadcast((NSEQ, D)))))
    t_small.append(ld.issue(nc.sync.dma_start(out=sel4[:].rearrange("p a b -> p (a b)"), in_=sel4_d)))
    t_small = t_small[-1]

    wbd32 = sza[:, :, :].rearrange("p a (c m) -> p a c m", c=4)
    t0 = POOL.done(nc.gpsimd.memset(wbd32, 0.0))
    SP.wait(t0)
    ld2 = DmaSem(nc, "setup2")
    tw = None
    for which, src in ((0, wr_d), (1, wi_d)):
        for g in range(8):
            ci, g2 = g // 2, g % 2
            tw = ld2.issue(nc.sync.dma_start(
                out=wbd32[g2 * 64:(g2 + 1) * 64, which, ci, g2 * 64:(g2 + 1) * 64], in_=src[g]))
    POOL.wait(tw, t_small)
    t1 = POOL.done(nc.gpsimd.tensor_copy(wr_bd[:], wbd32[:, 0]))
    t1 = POOL.done(nc.gpsimd.tensor_copy(wi_bd[:], wbd32[:, 1]))
    t1 = POOL.done(nc.gpsimd.tensor_copy(tri[:], tri32[:]))
    t1 = POOL.done(nc.gpsimd.memset(onesm[:], 1.0 / 512.0))
    t_pool_consts = t1

    cact = sb("cact", [128, 8, NSEQ])
    ctmp_s = sb("ctmp_s", [128, 8, NSEQ])
    lam_e = sb("lam_e", [128, 4])
    lam_p = sb("lam_p", [128, 4])
    ACT.wait(t_small)
    ta = ACT.done(nc.scalar.activation(out=ctmp_s[:], in_=cTs[:], func=AF.Exp, scale=-1.0))
    tb = ACT.done(nc.scalar.activation(out=lam_e[:], in_=vecs[:, V_LAM:V_LAM + 4], func=AF.Exp, scale=-1.0))
    DVE.wait(ta, tb, t_small)
    td = DVE.done(nc.vector.tensor_scalar_add(ctmp_s[:], ctmp_s[:], 1.0))
    DVE.wait(td)
    td = DVE.done(nc.vector.reciprocal(ctmp_s[:], ctmp_s[:]))
    DVE.wait(td)
    td = DVE.done(nc.vector.tensor_mul(cact[:], cTs[:], ctmp_s[:]))
    td = DVE.done(nc.vector.tensor_scalar(out=lam_p[:], in0=lam_e[:], scalar1=-0.25, scalar2=1.0 / 3.0,
                                          op0=ALU.mult, op1=ALU.add))
    DVE.wait(td)
    td = DVE.done(nc.vector.tensor_mul(lam_p[:], lam_p[:], lam_e[:]))
    DVE.wait(td)
    td = DVE.done(nc.vector.tensor_scalar(out=lam_p[:], in0=lam_p[:], scalar1=-1.0, scalar2=0.5,
                                          op0=ALU.mult, op1=ALU.add))
    DVE.wait(td)
    td = DVE.done(nc.vector.tensor_mul(lam_p[:], lam_p[:], lam_e[:]))
    DVE.wait(td)
    td = DVE.done(nc.vector.tensor_scalar(out=lam_p[:], in0=lam_p[:], scalar1=-1.0, scalar2=1.0,
                                          op0=ALU.mult, op1=ALU.add))
    DVE.wait(td)
    td = DVE.done(nc.vector.tensor_mul(lam_p[:], lam_p[:], lam_e[:]))
    DVE.wait(td)
    td = DVE.done(nc.vector.tensor_scalar_mul(coef[:], lam_p[:], -8.0))
    td = DVE.done(nc.vector.tensor_scalar_mul(coef2[:], lam_p[:], -16.0))
    td = DVE.done(nc.vector.tensor_scalar_mul(nbr[:], vecs[:, V_BR:V_BR + 4], -1.0))
    td = DVE.done(nc.vector.tensor_scalar_mul(nbi[:], vecs[:, V_BI:V_BI + 4], -1.0))
    td = DVE.done(nc.vector.memset(vaug[:, :, :, HD:HD + 1], 1.0))
    t_vec_consts = td

    stg_sem = [DmaSem(nc, "stg%d" % i) for i in range(5)]
    stg_free = [None] * 5

    def stage_load(k, src_ap, cols):
        SP.wait(stg_free[k])
        return stg_sem[k].issue(nc.sync.dma_start(
            out=stg[k][:, 0:8 * cols].rearrange("p (k c) -> p k c", k=8),
            in_=src_ap.rearrange("(k p) c -> p k c", p=128)))

    CB = 128
    sidx = 0
    PE.wait(td, t_pool_consts)
    for fcol in range(16):
        k = sidx % 5
        sidx += 1
        tl = stage_load(k, wmod_d[:, fcol * 128:(fcol + 1) * 128], CB)
        bi_, bk, brel = banks.get()
        PE.wait(tl, brel)
        wv = stg[k][:, 0:8 * CB].rearrange("p (k c) -> p k c", k=8)
        for kc in range(8):
            mm = nc.tensor.matmul(bk[:, 0:NSEQ], wv[:, kc, :], cact[:, kc, :], start=(kc == 0), stop=(kc == 7))
        tp = PE.done(mm)
        stg_free[k] = tp
        DVE.wait(tp)
        fc = fcol % 8
        if fcol < 8:
            tdd = DVE.done(nc.vector.tensor_scalar(out=shT[:, fc, :], in0=bk[:, 0:NSEQ],
                                                   scalar1=vecs[:, V_BSH + fc:V_BSH + fc + 1], scalar2=None,
                                                   op0=ALU.add))
        else:
            tdd = DVE.done(nc.vector.tensor_scalar(out=gsT[:, fc, :], in0=bk[:, 0:NSEQ],
                                                   scalar1=vecs[:, V_BSC + fc:V_BSC + fc + 1], scalar2=1.0,
                                                   op0=ALU.add, op1=ALU.add))
            DVE.wait(tdd)
            tdd = DVE.done(nc.vector.tensor_scalar(out=gsT[:, fc, :], in0=gsT[:, fc, :],
                                                   scalar1=vecs[:, V_NG + fc:V_NG + fc + 1], scalar2=None,
                                                   op0=ALU.mult))
        banks.put(bi_, tdd)
    t_mod = tdd
    for gcol in range(8):
        k = sidx % 5
        sidx += 1
        tl = stage_load(k, wmod_d[:, 2 * D + gcol * 128:2 * D + (gcol + 1) * 128], CB)
        bi_, bk, brel = banks.get()
        PE.wait(tl, brel)
        wv = stg[k][:, 0:8 * CB].rearrange("p (k c) -> p k c", k=8)
        for kc in range(8):
            mm = nc.tensor.matmul(bk[0:NSEQ, 0:CB], cact[:, kc, :], wv[:, kc, :], start=(kc == 0), stop=(kc == 7))
        tp = PE.done(mm)
        stg_free[k] = tp
        DVE.wait(tp, t_small)
        tdd = DVE.done(nc.vector.tensor_add(gate_rows[:, gcol * 128:(gcol + 1) * 128],
                                            gate_rows[:, gcol * 128:(gcol + 1) * 128], bk[0:NSEQ, 0:CB]))
        banks.put(bi_, tdd)
    t_gate_rows = tdd

    cast_engs = [(ACT, lambda o, i: nc.scalar.copy(o, i)),
                 (DVE, lambda o, i: nc.vector.tensor_copy(o, i)),
                 (POOL, lambda o, i: nc.gpsimd.tensor_copy(o, i))]
    t_w = []
    nblk = 0
    for (dst, src, ncols) in ((win_sb, win_d, 3 * D), (wout_sb, wout_d, D)):
        for cb_ in range(ncols // CB):
            k = sidx % 5
            sidx += 1
            tl = stage_load(k, src[:, cb_ * CB:(cb_ + 1) * CB], CB)
            E, fn = cast_engs[nblk % 3]
            nblk += 1
            E.wait(tl)
            tcst = E.done(fn(dst[:, :, cb_ * CB:(cb_ + 1) * CB],
                             stg[k][:, 0:8 * CB].rearrange("p (k c) -> p k c", k=8)))
            stg_free[k] = tcst
            t_w.append(tcst)
    t_setup = [t_w[-1], t_w[-2], t_w[-3], t_mod, t_gate_rows, t_vec_consts, t_pool_consts] + [f for f in stg_free if f]
    for E in (PE, ACT, DVE, POOL, SP):
        E.wait(t_setup)

    xt_sem = [DmaSem(nc, "xt%d" % i) for i in range(NXB)]
    xt_free = [None] * NXB
    xr_sem = [DmaSem(nc, "xr%d" % i) for i in range(2)]
    xr_free = [None] * 2
    cs_sem = [DmaSem(nc, "cs%d" % i) for i in range(2)]
    cs_free = [None] * 2
    out_sem = [DmaSem(nc, "out%d" % i) for i in range(2)]
    st = dict(xt_n=0, xr_n=0, pt_n=0, chunk_n=0, ct_n=0, rot_n=0)
    PT_free = [None] * NPT
    free_tok = {}

    def fr(name):
        return free_tok.get(name)

    x_loaded = {}

    def issue_x_load(s, i):
        toks = []
        for tt in range(2):
            b = st["xt_n"] % NXB
            st["xt_n"] += 1
            SP.wait(xt_free[b])
            T = 2 * i + tt
            toks.append((b, xt_sem[b].issue(LZp.dma_start(out=xt[b][:], in_=x_d[s, T * 128:(T + 1) * 128, :]), SP, 524288)))
        x_loaded[(s, i)] = toks

    cs_loaded = {}

    def issue_cs_load(s, i):
        SP.wait(cs_free[0])
        tcs = cs_sem[0].issue(LZp.dma_start(out=cosb[0][:], in_=cos_d[:, i * BLK:(i + 1) * BLK]), SP, 131072)
        tcs = cs_sem[0].issue(LZp.dma_start(out=sinb[0][:], in_=sin_d[:, i * BLK:(i + 1) * BLK]), SP, 131072)
        cs_loaded[(s, i)] = tcs

    order = [(s, i) for s in range(NSEQ) for i in range(NB)]
    banks.free.remove(7)
    LZt, LZs, LZv, LZg, LZp = LazyEng(PE), LazyEng(ACT), LazyEng(DVE), LazyEng(POOL), LazyEng(SP)
    Sched.ops = []
    Sched.on = True

    class Cx:
        pass

    def emit_gate_bc(s):
        for half in range(2):
            bi_, bk, brel = banks.get()
            PE.wait(brel, t_gate_rows, t_pool_consts)
            tp = PE.done(LZt.matmul(bk[:, :], sel4[:, s, :], gate_rows[:, half * 512:(half + 1) * 512],
                                          start=True, stop=True))
            ACT.wait(tp, fr("gate_bc"))
            tg = ACT.done(LZs.copy(gate_bc[:, half * 512:(half + 1) * 512], bk[:, :]))
            banks.put(bi_, tg)
        free_tok["gate_bc_ready"] = tg

    def ef_A0(cx):
        s, i = cx.s, cx.i
        cx.xtoks = x_loaded.pop((s, i))
        cx.t_cs = cs_loaded.pop((s, i))
        xtoks = cx.xtoks
        if i == 0:
            POOL.wait([fr("xl%d" % c_) for c_ in range(4)])
            th = POOL.done(LZg.memset(xl[:, :, 0:3], 0.0))
            POOL.wait([fr("hc%d" % c_) for c_ in range(4)], [fr("rec%d" % c_) for c_ in range(2)])
            th2 = POOL.done(LZg.memset(hcarry[:], 0.0))
            free_tok["xl_halo_ready"] = th
            free_tok["hcarry_ready"] = th2
        t_xn = []
        for tt in range(2):
            b, tl = xtoks[tt]
            ACT.wait(tl, fr("junk"), fr("stat_in"))
            tq = ACT.done(LZs.activation(out=junk[:], in_=xt[b][:], func=AF.Square,
                                               accum_out=stat[:, tt:tt + 1]))
            free_tok["junk"] = tq
            t_xn.append(tq)
        ACT.wait(t_xn)
        tq = ACT.done(LZs.activation(out=stat[:, 2:4], in_=stat[:, 0:2], func=AF.Ln, scale=1.0 / D, bias=EPS))
        ACT.wait(tq)
        t_rstd = ACT.done(LZs.activation(out=stat[:, 4:6], in_=stat[:, 2:4], func=AF.Exp, scale=-0.5))
        t_xn2 = []
        for tt in range(2):
            b, tl = xtoks[tt]
            DVE.wait(t_rstd, tl)
            t_xn2.append(DVE.done(LZv.tensor_scalar(out=xt[b][:], in0=xt[b][:], scalar1=stat[:, 4 + tt:5 + tt],
                                                          scalar2=None, op0=ALU.mult)))
        free_tok["stat_in"] = t_xn2
        t_hT = []
        t_tr_all = []
        for fp in range(4):
            bi_, bk, brel = banks.get()
            PE.wait(brel, t_xn2)
            for f2 in range(2):
                fc = 2 * fp + f2
                for tt in range(2):
                    b, _ = xtoks[tt]
                    mm = LZt.transpose(bk[:, f2 * 256 + tt * 128:f2 * 256 + (tt + 1) * 128],
                                             xt[b][:, fc * 128:(fc + 1) * 128], ident[:])
            tp = PE.done(mm)
            t_tr_all.append(tp)
            evs = []
            for f2 in range(2):
                fc = 2 * fp + f2
                if fp % 2 == 0:
                    ACT.wait(tp, fr("hT"))
                    evs.append(ACT.done(LZs.activation(out=hT[:, fc, :], in_=bk[:, f2 * 256:(f2 + 1) * 256],
                                                             func=AF.Identity,
                                                             scale=gsT[:, fc, s:s + 1], bias=shT[:, fc, s:s + 1])))
                else:
                    DVE.wait(tp, fr("hT"))
                    evs.append(DVE.done(LZv.tensor_scalar(out=hT[:, fc, :], in0=bk[:, f2 * 256:(f2 + 1) * 256],
                                                                scalar1=gsT[:, fc, s:s + 1], scalar2=shT[:, fc, s:s + 1],
                                                                op0=ALU.mult, op1=ALU.add)))
            banks.put(bi_, evs)
            t_hT.extend(evs)
        for tt in range(2):
            xt_free[xtoks[tt][0]] = list(t_tr_all)
        cx.t_hT = t_hT
        cx.t_hT_rd = []
        cx.t_cs_rd = []
        if cx.oi + 1 < len(order):
            issue_x_load(*order[cx.oi + 1])

    def inproj_fm(col0, bk, half):
        for kc in range(8):
            mm_ = LZt.matmul(bk[:, half * 256:(half + 1) * 256], win_sb[:, kc, col0:col0 + 128], hT[:, kc, :],
                                   start=(kc == 0), stop=(kc == 7))
        return mm_

    def rope_tiles(cx, kind):
        s, i = cx.s, cx.i
        c0 = i * BLK
        colbase = 512 if kind == "k" else 0
        toks = []
        t_kmean = []
        for ht in range(4):
            bi_, bk, brel = banks.get()
            PE.wait(brel, cx.t_hT)
            tp = PE.done(inproj_fm(colbase + ht * 128, bk, 0))
            r = st["rot_n"] % 2
            st["rot_n"] += 1
            DVE.wait(tp, fr("kq32_%d" % r))
            tc = DVE.done(LZv.tensor_copy(kq32[r][:], bk[:, 0:256]))
            PE.wait(tc)
            tr = PE.done(LZt.matmul(bk[:, 256:512], rmat[:], kq32[r][:], start=True, stop=True))
            cx.t_hT_rd.append(tp)
            POOL.wait(tc, cx.t_cs, fr("rt1_%d" % r))
            tm1 = POOL.done(LZg.tensor_mul(rt1[r][:], kq32[r][:], cosb[0][:]))
            DVE.wait(tr, cx.t_cs, fr("rt2_%d" % r))
            tm2 = DVE.done(LZv.tensor_mul(rt2[r][:], bk[:, 256:512], sinb[0][:]))
            cx.t_cs_rd += [tm1, tm2]
            banks.put(bi_, tm2)
            free_tok["kq32_%d" % r] = [tm1, tr]
            DVE.wait(tm1, tm2)
            if kind == "k":
                DVE.wait(fr("krot"))
                t3 = DVE.done(LZv.tensor_add(krot[:], rt1[r][:], rt2[r][:]))
                free_tok["rt1_%d" % r] = t3
                free_tok["rt2_%d" % r] = t3
                ACT.wait(t3, fr("kT"))
                tk = ACT.done(LZs.copy(kT[:, ht, c0:c0 + BLK], krot[:]))
                toks.append(tk)
                DVE.wait(t3, fr("ksum"))
                t4 = DVE.done(LZv.tensor_reduce(out=ksum[:, ht:ht + 1], in_=krot[:], axis=AX.X, op=ALU.add))
                free_tok["krot"] = [tk, t4]
                DVE.wait(t4, fr("kmean"))
                t5 = DVE.done(LZv.tensor_scalar(out=kmean[:, ht, i:i + 1], in0=ksum[:, ht:ht + 1],
                                                      scalar1=1.0 / BLK, scalar2=None, op0=ALU.mult))
                free_tok["ksum"] = t5
                t_kmean.append(t5)
            else:
                DVE.wait(fr("qT"))
                t3 = DVE.done(LZv.tensor_add(qT[:, ht, :], rt1[r][:], rt2[r][:]))
                free_tok["rt1_%d" % r] = t3
                free_tok["rt2_%d" % r] = t3
                toks.append(t3)
        if kind == "k":
            cx.t_kT = toks
            cx.t_kmean = t_kmean
        else:
            cx.t_qT = toks

    def ef_K(cx):
        rope_tiles(cx, "k")

    def ef_V(cx):
        i = cx.i
        t_v = []
        for tt in range(2):
            T = 2 * i + tt
            bi_, bk, brel = banks.get()
            PE.wait(brel, cx.t_hT)
            for kc in range(8):
                mm = LZt.matmul(bk[:, :], hT[:, kc, tt * 128:(tt + 1) * 128], win_sb[:, kc, 1024:1536],
                                      start=(kc == 0), stop=(kc == 7))
            tp = PE.done(mm)
            cx.t_hT_rd.append(tp)
            DVE.wait(tp, fr("vaug"), fr("vaug_chain"))
            tv = DVE.done(LZv.tensor_copy(vaug[:, T, :, 0:HD], bk[:, :].rearrange("p (h d) -> p h d", h=NH)))
            banks.put(bi_, tv)
            free_tok["vaug_chain"] = tv
            t_v.append(tv)
        cx.t_v = t_v

    def ef_L(cx, ci):
        s, i = cx.s, cx.i
        p = ci % 2
        ylT = ylT2[cx.oi % 2]
        if ci == 0:
            cx.t_yp = []
        bi_, bk, brel = banks.get()
        PE.wait(brel, cx.t_hT)
        tpx = PE.done(inproj_fm(2048 + ci * 128, bk, 0))
        PE.wait(tpx, cx.t_hT)
        tpz = PE.done(inproj_fm(2560 + ci * 128, bk, 1))
        cx.t_hT_rd += [tpx, tpz]
        ACT.wait(tpx, tpz, fr("xl%d" % ci), fr("xl_halo_ready"))
        tx = ACT.done(LZs.copy(xl[:, ci, 3:3 + BLK], bk[:, 0:256]))
        ACT.wait(tpz, fr("uu%d" % p))
        tez = ACT.done(LZs.activation(out=uu[p][:], in_=bk[:, 256:512], func=AF.Exp, scale=-1.0))
        ACT.wait(tez)
        t8 = ACT.done(LZs.activation(out=uu[p][:], in_=uu[p][:], func=AF.Ln, bias=1.0))
        ACT.wait(t8)
        t8 = ACT.done(LZs.activation(out=uu[p][:], in_=uu[p][:], func=AF.Exp, scale=-1.0))
        DVE.wait(t8, tpz, fr("szl%d" % ci))
        t_szl = DVE.done(LZv.tensor_mul(szl[:, ci, :], bk[:, 256:512], uu[p][:]))
        banks.put(bi_, [tx, t_szl])
        free_tok["uu%d" % p] = t_szl
        POOL.wait(tx, fr("xc%d" % p), fr("xl%d" % ci), fr("xl_halo_ready"))
        t9 = POOL.done(LZg.tensor_scalar(out=xc[p][:], in0=xl[:, ci, 0:BLK],
                                               scalar1=vecs[:, V_CW + ci * 4:V_CW + ci * 4 + 1],
                                               scalar2=vecs[:, V_CB + ci:V_CB + ci + 1], op0=ALU.mult, op1=ALU.add))
        for w in range(1, 4):
            DVE.wait(t9)
            t9 = DVE.done(LZv.scalar_tensor_tensor(out=xc[p][:], in0=xl[:, ci, w:w + BLK],
                                                         scalar=vecs[:, V_CW + ci * 4 + w:V_CW + ci * 4 + w + 1],
                                                         in1=xc[p][:], op0=ALU.mult, op1=ALU.add))
        POOL.wait(t9, fr("xcb%d" % p))
        t10 = POOL.done(LZg.tensor_copy(xcb[p][:], xc[p][:]))
        POOL.wait(t9, tx)
        t_halo = POOL.done(LZg.tensor_copy(xl[:, ci, 0:3], xl[:, ci, BLK:BLK + 3]))
        free_tok["xl%d" % ci] = t_halo
        bi_, bk, brel = banks.get()
        PE.wait(brel, t10)
        LZt.matmul(bk[:, 0:256], wr_bd[:, ci, :], xcb[p][:], start=True, stop=True)
        tpg = PE.done(LZt.matmul(bk[:, 256:512], wi_bd[:, ci, :], xcb[p][:], start=True, stop=True))
        free_tok["xcb%d" % p] = tpg
        ACT.wait(tpg, fr("er%d" % p), fr("ei%d" % p))
        ter = ACT.done(LZs.activation(out=er[p], in_=bk[:, 0:256], func=AF.Exp, scale=-1.0,
                                            bias=nbr[:, ci:ci + 1]))
        ACT.wait(tpg, fr("ei%d" % p), fr("er%d" % p))
        tei = ACT.done(LZs.activation(out=ei[p], in_=bk[:, 256:512], func=AF.Exp, scale=-1.0,
                                            bias=nbi[:, ci:ci + 1]))
        banks.put(bi_, [ter, tei])
        ACT.wait(ter, tei)
        t11 = ACT.done(LZs.activation(out=eri[p][:, :, :], in_=eri[p][:, :, :], func=AF.Ln, bias=1.0))
        ACT.wait(t11)
        t11 = ACT.done(LZs.activation(out=eri[p][:, :, :], in_=eri[p][:, :, :], func=AF.Exp, scale=-1.0))
        t12 = t11
        ACT.wait(t11, fr("aa%d" % p))
        ta_ = ACT.done(LZs.activation(out=aa[p][:], in_=er[p], func=AF.Exp, scale=coef[:, ci:ci + 1]))
        POOL.wait(ta_, fr("a2_%d" % p))
        ta2 = POOL.done(LZg.tensor_mul(a2[p][:], aa[p][:], aa[p][:]))
        free_tok["er%d" % p] = ta_
        ACT.wait(ta2)
        ta2 = ACT.done(LZs.activation(out=a2[p][:], in_=a2[p][:], func=AF.Ln, scale=-1.0, bias=1.0))
        ACT.wait(ta2)
        tnrm = ACT.done(LZs.activation(out=a2[p][:], in_=a2[p][:], func=AF.Exp, scale=0.5))
        DVE.wait(t12, t9)
        t12 = DVE.done(LZv.tensor_mul(ei[p], ei[p], xc[p][:]))
        free_tok["xc%d" % p] = t12
        DVE.wait(t12, tnrm)
        t13 = DVE.done(LZv.tensor_mul(ei[p], ei[p], a2[p][:]))
        free_tok["a2_%d" % p] = t13
        DVE.wait(t13, ta_, fr("rec%d" % p), fr("hcarry_ready"), fr("hc%d" % ci))
        t14 = DVE.done(LZv.tensor_tensor_scan(out=rec[p][:], data0=aa[p][:], data1=ei[p],
                                                    initial=hcarry[:, ci:ci + 1], op0=ALU.mult, op1=ALU.add))
        free_tok["aa%d" % p] = t14
        free_tok["ei%d" % p] = t14
        DVE.wait(t14)
        t15 = DVE.done(LZv.tensor_copy(hcarry[:, ci:ci + 1], rec[p][:, BLK - 1:BLK]))
        free_tok["hc%d" % ci] = t15
        POOL.wait(t14, fr("sqb%d" % p))
        t16 = POOL.done(LZg.tensor_mul(sqb[p][:], rec[p][:], rec[p][:]))
        DVE.wait(t_szl, t14, fr("yp"), fr("attn"))
        t17 = DVE.done(LZv.scalar_tensor_tensor(out=yp[:, ci, :], in0=rec[p][:],
                                                      scalar=vecs[:, V_GL + ci:V_GL + ci + 1], in1=szl[:, ci, :],
                                                      op0=ALU.mult, op1=ALU.mult))
        free_tok["rec%d" % p] = [t16, t17, t15]
        free_tok["szl%d" % ci] = t17
        cx.t_yp.append(t17)
        if ci == 0:
            cx.sb_ = banks.get()
            PE.wait(cx.sb_[2])
        sbi, sbk, _ = cx.sb_
        PE.wait(t16, cx.t_stats_prev if ci > 0 else None)
        tps = PE.done(LZt.matmul(sbk[:, 0:256], onesm[:], sqb[p][:], start=(ci == 0), stop=(ci == 3)))
        cx.t_stats_prev = tps
        free_tok["sqb%d" % p] = tps
        if ci == 3:
            free_tok["xl_halo"] = t_halo
            free_tok["hcarry"] = t15
            ACT.wait(tps, fr("rstd_bc"))
            tl1 = ACT.done(LZs.activation(out=rstd_bc[:], in_=sbk[:, 0:256], func=AF.Ln, bias=EPS))
            banks.put(sbi, tl1)
            ACT.wait(tl1)
            tl2 = ACT.done(LZs.activation(out=rstd_bc[:], in_=rstd_bc[:], func=AF.Exp, scale=-0.5))
            t_ylT = []
            for c2 in range(4):
                E = POOL if c2 % 2 else DVE
                E.wait(tl2, cx.t_yp[c2], fr("ylT%d" % (cx.oi % 2)))
                fn = LZg.tensor_mul if c2 % 2 else LZv.tensor_mul
                t_ylT.append(E.done(fn(ylT[:, c2, :], yp[:, c2, :], rstd_bc[:])))
            free_tok["rstd_bc"] = t_ylT
            free_tok["yp"] = t_ylT
            cx.t_ylT = t_ylT

    def lf_Q(cx):
        rope_tiles(cx, "q")
        cs_free[0] = list(cx.t_cs_rd)
        if cx.oi + 1 < len(order):
            issue_cs_load(*order[cx.oi + 1])

    def lf_Z(cx):
        t_sza = []
        for tt in range(2):
            bi_, bk, brel = banks.get()
            PE.wait(brel, cx.t_hT)
            for kc in range(8):
                mm = LZt.matmul(bk[:, :], hT[:, kc, tt * 128:(tt + 1) * 128], win_sb[:, kc, 1536:2048],
                                      start=(kc == 0), stop=(kc == 7))
            tp = PE.done(mm)
            cx.t_hT_rd.append(tp)
            ACT.wait(tp, fr("ez%d" % tt))
            te = ACT.done(LZs.activation(out=ez[tt][:], in_=bk[:, :], func=AF.Exp, scale=-1.0))
            ACT.wait(te)
            t6 = ACT.done(LZs.activation(out=ez[tt][:], in_=ez[tt][:], func=AF.Ln, bias=1.0))
            ACT.wait(t6)
            t6 = ACT.done(LZs.activation(out=ez[tt][:], in_=ez[tt][:], func=AF.Exp, scale=-1.0))
            DVE.wait(t6, tp, fr("sza"))
            t7 = DVE.done(LZv.tensor_mul(sza[:, tt, :], bk[:, :], ez[tt][:]))
            free_tok["ez%d" % tt] = t7
            banks.put(bi_, t7)
            t_sza.append(t7)
        cx.t_sza = t_sza
        free_tok["hT"] = list(cx.t_hT_rd)

    def at_SEL(cx):
        i = cx.i
        cx.t_acc = []
        cx.t_PV_all = []
        cx.t_ST_all = []
        cx.t_tcm_all = []
        cx.t_sel = None
        if i < 4:
            return
        gb = [banks.get() for _ in range(2)]
        PE.wait(gb[0][2], gb[1][2], cx.t_qT, cx.t_kmean)
        for tt in range(2):
            for ht in range(4):
                for hh in range(2):
                    hb = hh * 64
                    col = (tt * 4 + ht) * 8
                    mm = LZt.matmul(gb[hh][1][:, col:col + i],
                                          qT[hb:hb + 64, ht, tt * 128:(tt + 1) * 128],
                                          kmean[hb:hb + 64, ht, 0:i], start=True, stop=True)
        tpg = PE.done(mm)
        g5 = g32[:, :, :, :].rearrange("p t (a b) j -> p t a b j", b=2)
        tg32s = []
        for hh in range(2):
            ACT.wait(tpg, fr("g32"))
            tg32 = ACT.done(LZs.copy(
                g5[:, :, :, hh, 0:i],
                gb[hh][1][:, 0:64].rearrange("p (t a j) -> p t a j", t=2, a=4)[:, :, :, 0:i]))
            banks.put(gb[hh][0], tg32)
            tg32s.append(tg32)
        t_sel = []
        for tt in range(2):
            gv = g32[:, tt, :, 0:i]
            DVE.wait(tg32s, fr("cmpb"))
            tc1 = DVE.done(LZv.tensor_tensor(out=cmpb[:, :, 0:i, 0:i],
                                                   in0=gv.unsqueeze(2).to_broadcast((128, NH, i, i)),
                                                   in1=gv.unsqueeze(3).to_broadcast((128, NH, i, i)),
                                                   op=ALU.is_gt))
            DVE.wait(tc1, fr("rank"))
            tc2 = DVE.done(LZv.tensor_reduce(out=rank[:, :, 0:i], in_=cmpb[:, :, 0:i, 0:i], axis=AX.X,
                                                   op=ALU.add))
            free_tok["cmpb"] = tc2
            DVE.wait(tc2, fr("selx"))
            tc3 = DVE.done(LZv.tensor_single_scalar(out=selx[:, tt, :, 0:i], in_=rank[:, :, 0:i], scalar=3.0,
                                                          op=ALU.is_lt))
            free_tok["rank"] = tc3
            t_sel.append(tc3)
        free_tok["g32"] = t_sel
        cx.t_sel = t_sel

    def at_P(cx, hp):
        i = cx.i
        steps = [i] + list(range(i))
        st_state = {}
        st_acc = {}
        accp = acc[:, :, 2 * hp:2 * hp + 2, :]

        def emit_st(j):
            own = (j == i)
            bx = [banks.get() for _ in range(2)]
            PE.wait(bx[0][2], bx[1][2], cx.t_qT, cx.t_kT)
            mm_ = None
            for ktl in range(2):
                kt = 2 * j + ktl
                qlo = 128 if (own and ktl == 1) else 0
                for hh in range(2):
                    hb = hh * 64
                    mm_ = LZt.matmul(bx[hh][1][:, ktl * 256 + qlo:(ktl + 1) * 256],
                                           kT[hb:hb + 64, hp, kt * 128:(kt + 1) * 128],
                                           qT[hb:hb + 64, hp, qlo:256], start=True, stop=True)
            tps_ = PE.done(mm_)
            cx.t_ST_all.append(tps_)
            res_ = []
            for hh in range(2):
                pb = st["pt_n"] % NPT
                st["pt_n"] += 1
                ACT.wait(tps_, PT_free[pb])
                if own:
                    te1_ = ACT.done(LZs.activation(out=PT[pb][:, 0, :], in_=bx[hh][1][:, 0:256], func=AF.Exp,
                                                   scale=0.125))
                    ACT.wait(tps_, PT_free[pb])
                    te_ = ACT.done(LZs.activation(out=PT[pb][:, 1, 128:256], in_=bx[hh][1][:, 384:512],
                                                        func=AF.Exp, scale=0.125))
                else:
                    te_ = ACT.done(LZs.activation(out=PT[pb][:, :, :].rearrange("p a q -> p (a q)"),
                                                        in_=bx[hh][1][:, :], func=AF.Exp, scale=0.125))
                banks.put(bx[hh][0], [te_, te1_] if own else te_)
                tok = te_
                if own:
                    POOL.wait(te1_, t_pool_consts)
                    tm1_ = POOL.done(LZg.tensor_mul(PT[pb][:, 0, 0:128], PT[pb][:, 0, 0:128], tri[:]))
                    POOL.wait(te_, t_pool_consts)
                    tm2_ = POOL.done(LZg.tensor_mul(PT[pb][:, 1, 128:256], PT[pb][:, 1, 128:256], tri[:]))
                    tok = [tm1_, tm2_, te1_, te_]
                res_.append((pb, tok))
            st_state[j] = res_

        def emit_pv(j):
            own = (j == i)
            (pbA, tokA), (pbB, tokB) = st_state.pop(j)
            zi, zk, zrel = banks.get()
            PE.wait(tokA, tokB, cx.t_v, zrel)
            mm_ = None
            for tt in range(2):
                for hh in range(2):
                    h = 2 * hp + hh
                    pb = (pbA, pbB)[hh]
                    g = tt * 2 + hh
                    dst = zk[:, g * 65:(g + 1) * 65]
                    ktls = [0] if (own and tt == 0) else [0, 1]
                    for n_, ktl in enumerate(ktls):
                        mm_ = LZt.matmul(dst, PT[pb][:, ktl, tt * 128:(tt + 1) * 128],
                                               vaug[:, 2 * j + ktl, h, :],
                                               start=(n_ == 0), stop=(n_ == len(ktls) - 1))
            tpv_ = PE.done(mm_)
            cx.t_PV_all.append(tpv_)
            PT_free[pbA] = tpv_
            PT_free[pbB] = tpv_
            zv = zk[:, 0:4 * 65].rearrange("p (t h d) -> p t h d", t=2, h=2)
            if own:
                ACT.wait(tpv_, fr("acc"))
                ta_ = ACT.done(LZs.copy(accp, zv))
                banks.put(zi, ta_)
            elif i <= 3:
                DVE.wait(tpv_, st_acc["tok"])
                ta_ = DVE.done(LZv.tensor_add(accp, accp, zv))
                banks.put(zi, ta_)
            else:
                cb2 = st["ct_n"] % 2
                st["ct_n"] += 1
                DVE.wait(tpv_, cx.t_sel, fr("ctmp%d" % cb2))
                tcm = DVE.done(LZv.tensor_tensor(
                    out=ctmp[cb2][:, :, :, :], in0=zv,
                    in1=selx[:, :, 2 * hp:2 * hp + 2, j].unsqueeze(3).to_broadcast((128, 2, 2, 65)),
                    op=ALU.mult))
                banks.put(zi, tcm)
                cx.t_tcm_all.append(tcm)
                POOL.wait(tcm, st_acc["tok"])
                ta_ = POOL.done(LZg.tensor_add(accp, accp, ctmp[cb2][:, :, :, :]))
                free_tok["ctmp%d" % cb2] = ta_
            st_acc["tok"] = ta_
            return tpv_

        tpv = None
        emit_st(steps[0])
        if len(steps) > 1:
            emit_st(steps[1])
        for n_ in range(len(steps)):
            tpv = emit_pv(steps[n_])
            if n_ + 2 < len(steps):
                emit_st(steps[n_ + 2])
        cx.t_acc.append(st_acc["tok"])
        if hp == 3:
            free_tok["qT"] = list(cx.t_ST_all)
            if i >= 4:
                free_tok["selx"] = list(cx.t_tcm_all)

    def bk_N(cx):
        t_ya = []
        for tt in range(2):
            DVE.wait(cx.t_acc, fr("rden"))
            tn1 = DVE.done(LZv.reciprocal(rden[:, tt, :], acc[:, tt, :, HD]))
            DVE.wait(tn1, fr("attn"), fr("yp"))
            tn2 = DVE.done(LZv.tensor_mul(attn[:, tt, :].rearrange("p (h d) -> p h d", h=NH),
                                                acc[:, tt, :, 0:HD],
                                                rden[:, tt, :].unsqueeze(2).to_broadcast((128, NH, HD))))
            ACT.wait(tn2, fr("junk"), fr("stat_a"))
            tn3 = ACT.done(LZs.activation(out=junk[:, 0:512], in_=attn[:, tt, :], func=AF.Square,
                                                accum_out=stat[:, 6 + tt:7 + tt]))
            free_tok["junk"] = tn3
            ACT.wait(tn3)
            tn4 = ACT.done(LZs.activation(out=stat[:, 8 + tt:9 + tt], in_=stat[:, 6 + tt:7 + tt], func=AF.Ln,
                                                scale=1.0 / 512.0, bias=EPS))
            ACT.wait(tn4)
            tn5 = ACT.done(LZs.activation(out=stat[:, 10 + tt:11 + tt], in_=stat[:, 8 + tt:9 + tt], func=AF.Exp,
                                                scale=-0.5))
            DVE.wait(tn5, cx.t_sza[tt])
            tn6 = DVE.done(LZv.scalar_tensor_tensor(out=attn[:, tt, :], in0=attn[:, tt, :],
                                                          scalar=stat[:, 10 + tt:11 + tt], in1=sza[:, tt, :],
                                                          op0=ALU.mult, op1=ALU.mult))
            t_ya.append(tn6)
        free_tok["acc"] = t_ya
        free_tok["rden"] = t_ya
        free_tok["sza"] = t_ya
        free_tok["stat_a"] = t_ya
        t_yaT = []
        t_tr2_all = []
        for mp in range(2):
            bi_, bk, brel = banks.get()
            PE.wait(brel, t_ya, t_pool_consts)
            for m2 in range(2):
                m = 2 * mp + m2
                for tt in range(2):
                    mm = LZt.transpose(bk[:, m2 * 256 + tt * 128:m2 * 256 + (tt + 1) * 128],
                                             attn[:, tt, m * 128:(m + 1) * 128], ident[:])
            tp = PE.done(mm)
            evs = []
            for m2 in range(2):
                m = 2 * mp + m2
                ACT.wait(tp, fr("yaT"))
                evs.append(ACT.done(LZs.activation(out=yaT[:, m, :], in_=bk[:, m2 * 256:(m2 + 1) * 256],
                                                         func=AF.Identity, scale=vecs[:, V_GA + m:V_GA + m + 1])))
            banks.put(bi_, evs)
            t_yaT.extend(evs)
            t_tr2_all.append(tp)
        free_tok["attn"] = t_tr2_all
        cx.t_yaT = t_yaT

    def bk_O(cx):
        s, i = cx.s, cx.i
        ylT = ylT2[cx.oi % 2]
        t_oproj = []
        t_to1 = []
        for tt in range(2):
            T = 2 * i + tt
            rb = st["xr_n"] % 2
            st["xr_n"] += 1
            SP.wait(xr_free[rb])
            t_xr = xr_sem[rb].issue(LZp.dma_start(out=xr[rb][:], in_=x_d[s, T * 128:(T + 1) * 128, :]), SP, 524288)
            t_res = []
            for half in range(2):
                bi_, bk, brel = banks.get()
                PE.wait(brel, cx.t_yaT, cx.t_ylT)
                for kc in range(8):
                    lhs = yaT[:, kc, tt * 128:(tt + 1) * 128] if kc < 4 else ylT[:, kc - 4, tt * 128:(tt + 1) * 128]
                    mm = LZt.matmul(bk[:, :], lhs, wout_sb[:, kc, half * 512:(half + 1) * 512],
                                          start=(kc == 0), stop=(kc == 7))
                tp = PE.done(mm)
                t_oproj.append(tp)
                ob2 = half
                DVE.wait(tp, fr("ez%d" % ob2), fr("gate_bc_ready"))
                to1 = DVE.done(LZv.tensor_mul(otmp[ob2][:], bk[:, :], gate_bc[:, half * 512:(half + 1) * 512]))
                banks.put(bi_, to1)
                t_to1.append(to1)
                DVE.wait(to1, t_xr)
                to2 = DVE.done(LZv.tensor_add(xr[rb][:, half * 512:(half + 1) * 512],
                                              xr[rb][:, half * 512:(half + 1) * 512], otmp[ob2][:]))
                free_tok["ez%d" % ob2] = to2
                t_res.append(to2)
            ACT.wait(t_res, fr("junk"), fr("stat_f"))
            tf1 = ACT.done(LZs.activation(out=junk[:], in_=xr[rb][:], func=AF.Square,
                                                accum_out=stat[:, 12:13]))
            free_tok["junk"] = tf1
            ACT.wait(tf1)
            tf2 = ACT.done(LZs.activation(out=stat[:, 13:14], in_=stat[:, 12:13], func=AF.Ln, scale=1.0 / D, bias=EPS))
            ACT.wait(tf2)
            tf3 = ACT.done(LZs.activation(out=stat[:, 14:15], in_=stat[:, 13:14], func=AF.Exp, scale=-0.5))
            DVE.wait(tf3, t_res)
            tf4 = DVE.done(LZv.scalar_tensor_tensor(out=xr[rb][:], in0=xr[rb][:], scalar=stat[:, 14:15],
                                                          in1=fgain_bc[:], op0=ALU.mult, op1=ALU.mult))
            free_tok["stat_f"] = tf4
            SP.wait(tf4)
            t_out = out_sem[rb].issue(LZp.dma_start(out=out_d[s, T * 128:(T + 1) * 128, :], in_=xr[rb][:]), SP, 524288)
            xr_free[rb] = t_out
        free_tok["yaT"] = list(t_oproj)
        free_tok["ylT%d" % (cx.oi % 2)] = list(t_oproj)
        if i == NB - 1:
            free_tok["gate_bc"] = [tf4] + t_to1
            if s + 1 < NSEQ:
                emit_gate_bc(s + 1)

    def make_cx(oi):
        cx = Cx()
        cx.oi = oi
        cx.s, cx.i = order[oi]
        return cx

    def ef_pieces(cx):
        return [lambda: ef_A0(cx), lambda: ef_K(cx), lambda: ef_V(cx), lambda: ef_L(cx, 0), lambda: ef_L(cx, 1),
                lambda: ef_L(cx, 2), lambda: ef_L(cx, 3)]

    def at_pieces(cx):
        return [lambda: at_SEL(cx), lambda: at_P(cx, 0), lambda: at_P(cx, 1), lambda: at_P(cx, 2), lambda: at_P(cx, 3)]

    emit_gate_bc(0)
    issue_x_load(*order[0])
    issue_cs_load(*order[0])
    cur = make_cx(0)
    for f in ef_pieces(cur):
        f()
    lf_Q(cur)
    lf_Z(cur)
    for oi in range(len(order)):
        nxt = make_cx(oi + 1) if oi + 1 < len(order) else None
        A = at_pieces(cur)
        if nxt is None:
            for f in A:
                f()
        elif nxt.i == 0:
            for f in A:
                f()
            free_tok["kT"] = list(cur.t_ST_all)
            free_tok["vaug"] = list(cur.t_PV_all)
            free_tok["kmean"] = list(cur.t_ST_all) + list(cur.t_PV_all)
            for f in ef_pieces(nxt):
                f()
        else:
            E = ef_pieces(nxt)
            for f in (E[0], A[0], A[1], E[1], A[2], E[2], E[3], A[3], E[4], E[5], A[4], E[6]):
                f()
        bk_N(cur)
        if nxt is not None:
            lf_Q(nxt)
            lf_Z(nxt)
        bk_O(cur)
        cur = nxt

    for E_ in (PE, ACT, DVE, POOL, SP):
        assert not E_.p_calls and not E_.p_waits, E_.name
    scratch_bank = banks.t[7]

    def _pe_fill():
        nc.tensor.matmul(scratch_bank[:, :], win_sb[:, 0, 0:128], win_sb[:, 1, 0:512], start=True, stop=True)

    span_est = run_schedule((PE, ACT, DVE, POOL, SP), pe_fill=_pe_fill if PE_FILL else None)
    nc.sched_span_us = span_est
    for o_ in out_sem:
        SP.wait((o_.sem, o_.n, o_.key))
    return nc


_CACHE = {}


def _consts(S):
    pos = np.arange(S, dtype=np.float32)
    inv_freq = (np.float32(10000.0) ** (-np.arange(0, HD, 2, dtype=np.float32) / np.float32(HD))).astype(np.float32)
    ang = (pos[:, None] * inv_freq[None, :]).astype(np.float32)
    cos = np.cos(ang).astype(np.float32).T
    sin = np.sin(ang).astype(np.float32).T
    cosT = np.tile(cos, (4, 1))
    sinS = np.concatenate([-sin, sin, -sin, sin], axis=0)
    ident = np.eye(128, dtype=np.float32)
    rmat = np.zeros((128, 128), np.float32)
    for d in range(128):
        partner = (d // 64) * 64 + ((d % 64) + 32) % 64
        rmat[partner, d] = 1.0
    tri = np.triu(np.ones((128, 128), np.float32))
    return dict(cosT=np.ascontiguousarray(cosT), sinS=np.ascontiguousarray(sinS), ident=ident, rmat=rmat, tri=tri)


def _sel4(nseq):
    m = np.zeros((nseq, nseq, 128), np.float32)
    for b in range(nseq):
        m[b, b, :] = 1.0
    return m.reshape(nseq, nseq * 128)


def _fm(v, nchunk):
    return np.ascontiguousarray(np.asarray(v, np.float32).reshape(nchunk, 128).T)


def make_in_maps(inputs, n_cores, nseq, S):
    x = np.asarray(inputs["x"], np.float32)
    c = np.asarray(inputs["c"], np.float32)
    w_mod = np.ascontiguousarray(np.asarray(inputs["w_mod"], np.float32)[0])
    b_mod = np.asarray(inputs["b_mod"], np.float32)[0]
    vec = np.zeros((128, NVEC), np.float32)
    vec[:, 0:8] = _fm(inputs["norm_gain"][0], 8)
    vec[:, 8:16] = _fm(b_mod[0:D], 8)
    vec[:, 16:24] = _fm(b_mod[D:2 * D], 8)
    cw = np.asarray(inputs["conv_w"], np.float32)[0]
    vec[:, 24:40] = np.ascontiguousarray(cw.reshape(4, 4, 128).transpose(2, 1, 0)).reshape(128, 16)
    vec[:, 40:44] = _fm(inputs["conv_b"][0], 4)
    vec[:, 44:48] = _fm(inputs["lru_lambda"][0], 4)
    vec[:, 48:52] = _fm(np.asarray(inputs["b_rgate"], np.float32)[0].reshape(-1), 4)
    vec[:, 52:56] = _fm(np.asarray(inputs["b_igate"], np.float32)[0].reshape(-1), 4)
    vec[:, 56:60] = _fm(inputs["lru_out_gain"][0], 4)
    vec[:, 60:64] = _fm(inputs["attn_out_gain"][0], 4)
    common = dict(
        w_mod=w_mod,
        bgate=np.ascontiguousarray(b_mod[None, 2 * D:3 * D]),
        w_in=np.ascontiguousarray(np.asarray(inputs["w_in"], np.float32)[0]),
        w_out=np.ascontiguousarray(np.asarray(inputs["w_out"], np.float32)[0]),
        w_r=np.ascontiguousarray(np.asarray(inputs["w_rgate"], np.float32)[0]),
        w_i=np.ascontiguousarray(np.asarray(inputs["w_igate"], np.float32)[0]),
        vecs=vec,
        fgain=np.ascontiguousarray(np.asarray(inputs["final_gain"], np.float32)[None, :]),
    )
    common.update(_consts(S))
    common["sel4"] = _sel4(nseq)
    maps = []
    for k in range(n_cores):
        xs = np.ascontiguousarray(x[k * nseq:(k + 1) * nseq, :S])
        cs = c[k * nseq:(k + 1) * nseq]
        cT = np.ascontiguousarray(cs.reshape(nseq, 8, 128).transpose(2, 1, 0))
        m = dict(common)
        m["x"] = xs
        m["cT"] = cT
        maps.append(m)
    return maps


def kernel(**inputs):
    key = (NSEQ_FULL, S_FULL)
    if key not in _CACHE:
        _CACHE[key] = build_program(NSEQ_FULL, S_FULL)
    nc = _CACHE[key]
    maps = make_in_maps(inputs, N_CORES, NSEQ_FULL, S_FULL)
    res = run_bass_kernel_spmd(nc, maps, core_ids=list(range(N_CORES)))
    out = np.concatenate([np.asarray(r["out"], np.float32) for r in res.results], axis=0)
    return out
```

```python
import numpy as np
import concourse.bass as bass
import concourse.mybir as mybir
from concourse.bass_utils import run_bass_kernel_spmd

F32 = mybir.dt.float32
BF16 = mybir.dt.bfloat16
AF = mybir.ActivationFunctionType
ALU = mybir.AluOpType
AX = mybir.AxisListType

N_CORES = 8
D = 1024
S_FULL = 2048
NSEQ_FULL = 4
BLK = 256
NH = 8
HD = 64
EPS = 1e-6
NVEC = 64
PE_FILL = True
CP_W = 0.01


class Op:
    __slots__ = ("eng", "deps", "calls", "dur", "idx", "seq", "start", "end", "dma", "lat")


class Sched:
    ops = []
    on = False


class Eng:
    def __init__(self, nc, eng, name):
        self.nc = nc
        self.e = eng
        self.name = name
        self.sem = nc.alloc_semaphore("ms_" + name)
        self.n = 0
        self.seen = {}
        self.p_waits = []
        self.p_calls = []

    def wait(self, *toks):
        if Sched.on:
            self.p_waits.extend(_flat(toks))
            return
        for t in _flat(toks):
            self.raw_wait(_resolve(t))

    def raw_wait(self, t):
        sem, val, key = t
        if self.seen.get(key, 0) >= val:
            return
        self.e.wait_ge(sem, val)
        self.seen[key] = val

    def done(self, inst):
        if Sched.on:
            assert self.p_calls and self.p_calls[-1] is inst
            op = Op()
            op.eng, op.deps, op.calls = self, self.p_waits, self.p_calls
            op.dma = None
            op.dur = _est_dur(self.name, op.calls)
            op.lat = 0.0
            op.idx = len(Sched.ops)
            op.seq = None
            self.p_waits, self.p_calls = [], []
            Sched.ops.append(op)
            return op
        self.n += 1
        inst.then_inc(self.sem, 1)
        return (self.sem, self.n, self.name)


class LazyEng:
    def __init__(self, E):
        self._E = E

    def __getattr__(self, name):
        E = self._E

        def f(*a, **k):
            call = (name, a, k)
            E.p_calls.append(call)
            return call
        return f


def _resolve(t):
    if isinstance(t, Op):
        assert t.seq is not None, "dependency emitted after its consumer"
        if t.dma is not None:
            return (t.dma.sem, t.seq, t.dma.key)
        return (t.eng.sem, t.seq, t.eng.name)
    return t


def _ap_free(ap):
    n = 1
    for d in list(ap.shape)[1:]:
        n *= int(d)
    return n


def _est_dur(eng, calls):
    t = 0.0
    for (name, a, k) in calls:
        out = k.get("out", a[0] if a else None)
        n = _ap_free(out) if out is not None and hasattr(out, "shape") else 64
        if eng == "pe":
            if name == "transpose":
                t += 0.21
            else:
                lhs = k.get("lhsT", a[1] if len(a) > 1 else None)
                f32 = lhs is not None and lhs.dtype == F32
                rows = int(lhs.shape[0]) if lhs is not None else 128
                c = n / 2400.0
                if f32:
                    c *= 3.7
                elif rows <= 64:
                    c *= 0.56
                t += max(c, 0.035)
        elif eng == "act":
            t += 0.24 + n * 0.0005
        elif eng == "dve":
            t += 0.07 + n * 0.00105
        elif eng == "pool":
            t += 0.12 + n * 0.0021
        else:
            t += 0.1
    return t


def run_schedule(engs, pe_fill=None):
    ops = Sched.ops
    Sched.on = False
    per = {}
    for op in ops:
        per.setdefault(op.eng.name, []).append(op)
    tail = [0.0] * len(ops)
    for op in reversed(ops):
        t_ = tail[op.idx] + (op.lat if op.dma is not None else op.dur)
        tail[op.idx] = t_
        for d in op.deps:
            if isinstance(d, Op) and tail[d.idx] < t_ + 0.2:
                tail[d.idx] = t_ + 0.2 - 0.0
    pos = {e: 0 for e in per}
    free_t = {e: 0.0 for e in per}
    done_flag = [False] * len(ops)
    WIN = 64
    order = []
    remaining = len(ops)
    while remaining:
        best = None
        best_t = None
        best_key = None
        for e, lst in per.items():
            p = pos[e]
            while p < len(lst) and done_flag[lst[p].idx]:
                p += 1
            pos[e] = p
            win = 1 if e == "sp" else WIN
            cnt = 0
            q = p
            while q < len(lst) and cnt < win:
                op = lst[q]
                q += 1
                if done_flag[op.idx]:
                    continue
                cnt += 1
                ok = True
                t = free_t[e]
                for d in op.deps:
                    if isinstance(d, Op):
                        if not done_flag[d.idx]:
                            ok = False
                            break
                        te = d.end + (0.03 if d.eng is op.eng else 0.2)
                        if te > t:
                            t = te
                if not ok:
                    continue
                key = t - CP_W * tail[op.idx]
                if best is None or key < best_key - 1e-9 or (abs(key - best_key) <= 1e-9 and op.idx < best.idx):
                    best, best_t, best_key = op, t, key
        assert best is not None, "scheduler deadlock"
        best.start = best_t
        if best.dma is not None:
            best.end = best_t + best.lat
            free_t[best.eng.name] = best_t + 0.08
        else:
            best.end = best_t + best.dur
            free_t[best.eng.name] = best.end
        done_flag[best.idx] = True
        order.append(best)
        remaining -= 1
    order.sort(key=lambda o: (o.start, o.idx))
    pe_free = 0.0
    n_fill = 0
    for op in order:
        E = op.eng
        if E.name == "pe":
            gap = op.start - pe_free
            if pe_fill is not None and gap > 0.3:
                for _ in range(min(int((gap - 0.08) / 0.215), 80)):
                    pe_fill()
                    n_fill += 1
            pe_free = op.end
        for d in op.deps:
            E.raw_wait(_resolve(d))
        inst = None
        for (name, a, k) in op.calls:
            inst = getattr(E.e, name)(*a, **k)
        if op.dma is not None:
            inst.then_inc(op.dma.sem, 16)
        else:
            E.n += 1
            inst.then_inc(E.sem, 1)
            op.seq = E.n
    span = max(o.end for o in order) if order else 0.0
    Sched.ops = []
    return span


def _flat(toks):
    out = []
    for t in toks:
        if t is None:
            continue
        if isinstance(t, Op):
            out.append(t)
        elif isinstance(t, (list, tuple)) and len(t) == 3 and isinstance(t[2], str):
            out.append(t)
        elif isinstance(t, (list, tuple)):
            out.extend(_flat(t))
        else:
            raise TypeError(t)
    return out


class DmaSem:
    _cnt = 0

    def __init__(self, nc, name):
        self.sem = nc.alloc_semaphore("dma_" + name)
        self.n = 0
        DmaSem._cnt += 1
        self.key = "dma_%s_%d" % (name, DmaSem._cnt)

    def issue(self, inst, eng=None, nbytes=0):
        self.n += 16
        if Sched.on:
            assert eng.p_calls and eng.p_calls[-1] is inst
            op = Op()
            op.eng, op.deps, op.calls = eng, eng.p_waits, eng.p_calls
            op.dma = self
            op.seq = self.n
            op.dur = 0.08
            op.lat = 2.2 + nbytes / 150e3
            op.idx = len(Sched.ops)
            eng.p_waits, eng.p_calls = [], []
            Sched.ops.append(op)
            return op
        inst.then_inc(self.sem, 16)
        return (self.sem, self.n, self.key)


class Banks:
    def __init__(self, nc):
        self.t = [nc.alloc_psum_tensor("bank%d" % i, [128, 512], F32) for i in range(8)]
        self.free = list(range(8))
        self.rel = {i: None for i in range(8)}

    def get(self):
        assert self.free, "out of PSUM banks"
        i = self.free.pop(0)
        return i, self.t[i], self.rel[i]

    def put(self, i, *toks):
        self.rel[i] = _flat(toks)
        self.free.append(i)


def build_program(NSEQ=NSEQ_FULL, S=S_FULL, stop_after=None):
    NB = S // BLK
    NT = S // 128
    nc = bass.Bass("TRN2", target_bir_lowering=False)

    def din(name, shape, dt=F32):
        return nc.dram_tensor(name, list(shape), dt, kind="ExternalInput").ap()

    x_d = din("x", [NSEQ, S, D])
    cT_d = din("cT", [128, 8, NSEQ])
    wmod_d = din("w_mod", [D, 3 * D])
    bgate_d = din("bgate", [1, D])
    win_d = din("w_in", [D, 3 * D])
    wout_d = din("w_out", [D, D])
    wr_d = din("w_r", [8, 64, 64])
    wi_d = din("w_i", [8, 64, 64])
    vec_d = din("vecs", [128, NVEC])
    fg_d = din("fgain", [1, D])
    cos_d = din("cosT", [128, S])
    sin_d = din("sinS", [128, S])
    id_d = din("ident", [128, 128])
    rm_d = din("rmat", [128, 128])
    tri_d = din("tri", [128, 128])
    sel4_d = din("sel4", [NSEQ, NSEQ * 128])
    out_d = nc.dram_tensor("out", [NSEQ, S, D], F32, kind="ExternalOutput").ap()

    PE = Eng(nc, nc.tensor, "pe")
    ACT = Eng(nc, nc.scalar, "act")
    DVE = Eng(nc, nc.vector, "dve")
    POOL = Eng(nc, nc.gpsimd, "pool")
    SP = Eng(nc, nc.sync, "sp")
    banks = Banks(nc)

    def sb(name, shape, dt=F32):
        return nc.alloc_sbuf_tensor("s_" + name, list(shape), dt)

    win_sb = sb("win_sb", [128, 8, 3 * D], BF16)
    wout_sb = sb("wout_sb", [128, 8, D], BF16)
    kT = sb("kT", [128, 4, S], BF16)
    vaug = sb("vaug", [128, NT, NH, HD + 1], BF16)
    ident = sb("ident", [128, 128])
    rmat = sb("rmat", [128, 128])
    onesm = sb("onesm", [128, 128])
    tri = sb("tri", [128, 128], BF16)
    tri32 = sb("tri32", [128, 128])
    wr_bd = sb("wr_bd", [128, 4, 128], BF16)
    wi_bd = sb("wi_bd", [128, 4, 128], BF16)
    vecs = sb("vecs", [128, NVEC])
    gsT = sb("gsT", [128, 8, NSEQ])
    shT = sb("shT", [128, 8, NSEQ])
    coef = sb("coef", [128, 4])
    coef2 = sb("coef2", [128, 4])
    nbr = sb("nbr", [128, 4])
    nbi = sb("nbi", [128, 4])
    gate_rows = sb("gate_rows", [NSEQ, D])
    sel4 = sb("sel4", [NSEQ, NSEQ, 128])
    gate_bc = sb("gate_bc", [128, D])
    fgain_bc = sb("fgain_bc", [128, D])
    kmean = sb("kmean", [128, 4, 8], BF16)
    hcarry = sb("hcarry", [128, 4])
    NXB = 2
    xt = [sb("xt%d" % i, [128, D]) for i in range(NXB)]
    xr = [sb("xr%d" % i, [128, D]) for i in range(2)]
    junk = sb("junk", [128, D], BF16)
    stat = sb("stat", [128, 16])
    hT = sb("hT", [128, 8, BLK], BF16)
    kq32 = [sb("kq32_%d" % i, [128, BLK]) for i in range(2)]
    rt1 = [sb("rt1_%d" % i, [128, BLK]) for i in range(2)]
    rt2 = [sb("rt2_%d" % i, [128, BLK]) for i in range(2)]
    krot = sb("krot", [128, BLK])
    ksum = sb("ksum", [128, 4])
    qT = sb("qT", [128, 4, BLK], BF16)
    cosb = [sb("cosb0", [128, BLK])] * 2
    sinb = [sb("sinb0", [128, BLK])] * 2
    ez = [sb("ez%d" % i, [128, 512]) for i in range(2)]
    sza = sb("sza", [128, 2, 512])
    xl = sb("xl", [128, 4, BLK + 3])
    xc = [sb("xc%d" % i, [128, BLK]) for i in range(2)]
    xcb = [sb("xcb%d" % i, [128, BLK], BF16) for i in range(2)]
    eri = [sb("eri%d" % i, [128, 2, BLK]) for i in range(2)]
    er = [eri[i][:, 0, :] for i in range(2)]
    ei = [eri[i][:, 1, :] for i in range(2)]
    aa = [sb("aa%d" % i, [128, BLK]) for i in range(2)]
    a2 = [sb("a2_%d" % i, [128, BLK]) for i in range(2)]
    uu = [sb("uu%d" % i, [128, BLK]) for i in range(2)]
    rec = [sb("rec%d" % i, [128, BLK]) for i in range(2)]
    sqb = [sb("sqb%d" % i, [128, BLK]) for i in range(2)]
    rstd_bc = sb("rstd_bc", [128, BLK])
    ylT2 = [sb("ylT%d" % i, [128, 4, BLK], BF16) for i in range(2)]
    NPT = 4
    PT = [sb("PT%d" % i, [128, 2, BLK], BF16) for i in range(NPT)]
    accb = sb("accb", [128, 2 * NH * (HD + 1)])
    acc = accb[:, :].rearrange("p (t h d) -> p t h d", t=2, h=NH)
    szl = sb("szl", [128, 4, BLK])
    ctmp = [sb("ctmp%d" % i, [128, 2, 2, HD + 1]) for i in range(2)]
    rden = sb("rden", [128, 2, NH])
    ypat = sb("ypat", [128, 1024])
    attn = ypat[:, :].rearrange("p (t f) -> p t f", t=2)
    yp = ypat[:, :].rearrange("p (c t) -> p c t", c=4)
    yaT = sb("yaT", [128, 4, BLK], BF16)
    g32 = sb("g32", [128, 2, NH, 8])
    cmpb = sb("cmpb", [128, NH, 8, 8])
    rank = sb("rank", [128, NH, 8])
    selx = sb("selx", [128, 2, NH, 8])
    otmp = ez

    V_NG, V_BSH, V_BSC, V_CW, V_CB, V_LAM, V_BR, V_BI, V_GL, V_GA = 0, 8, 16, 24, 40, 44, 48, 52, 56, 60

    stg = [xt[0], xt[1], xr[0], xr[1], ypat]
    ld = DmaSem(nc, "setup")
    t_small = []
    for (dst, src) in ((ident, id_d), (rmat, rm_d), (tri32, tri_d), (vecs, vec_d)):
        t_small.append(ld.issue(nc.sync.dma_start(out=dst[:], in_=src)))
    cTs = sb("cTs", [128, 8, NSEQ])
    t_small.append(ld.issue(nc.sync.dma_start(out=cTs[:], in_=cT_d)))
    t_small.append(ld.issue(nc.sync.dma_start(out=fgain_bc[:], in_=fg_d.to_broadcast((128, D)))))
    t_small.append(ld.issue(nc.sync.dma_start(out=gate_rows[:], in_=bgate_d.to_broadcast((NSEQ, D)))))
    t_small.append(ld.issue(nc.sync.dma_start(out=sel4[:].rearrange("p a b -> p (a b)"), in_=sel4_d)))
    t_small = t_small[-1]

    wbd32 = sza[:, :, :].rearrange("p a (c m) -> p a c m", c=4)
    t0 = POOL.done(nc.gpsimd.memset(wbd32, 0.0))
    SP.wait(t0)
    ld2 = DmaSem(nc, "setup2")
    tw = None
    for which, src in ((0, wr_d), (1, wi_d)):
        for g in range(8):
            ci, g2 = g // 2, g % 2
            tw = ld2.issue(nc.sync.dma_start(
                out=wbd32[g2 * 64:(g2 + 1) * 64, which, ci, g2 * 64:(g2 + 1) * 64], in_=src[g]))
    POOL.wait(tw, t_small)
    t1 = POOL.done(nc.gpsimd.tensor_copy(wr_bd[:], wbd32[:, 0]))
    t1 = POOL.done(nc.gpsimd.tensor_copy(wi_bd[:], wbd32[:, 1]))
    t1 = POOL.done(nc.gpsimd.tensor_copy(tri[:], tri32[:]))
    t1 = POOL.done(nc.gpsimd.memset(onesm[:], 1.0 / 512.0))
    t_pool_consts = t1

    cact = sb("cact", [128, 8, NSEQ])
    ctmp_s = sb("ctmp_s", [128, 8, NSEQ])
    lam_e = sb("lam_e", [128, 4])
    lam_p = sb("lam_p", [128, 4])
    ACT.wait(t_small)
    ta = ACT.done(nc.scalar.activation(out=ctmp_s[:], in_=cTs[:], func=AF.Exp, scale=-1.0))
    tb = ACT.done(nc.scalar.activation(out=lam_e[:], in_=vecs[:, V_LAM:V_LAM + 4], func=AF.Exp, scale=-1.0))
    DVE.wait(ta, tb, t_small)
    td = DVE.done(nc.vector.tensor_scalar_add(ctmp_s[:], ctmp_s[:], 1.0))
    DVE.wait(td)
    td = DVE.done(nc.vector.reciprocal(ctmp_s[:], ctmp_s[:]))
    DVE.wait(td)
    td = DVE.done(nc.vector.tensor_mul(cact[:], cTs[:], ctmp_s[:]))
    td = DVE.done(nc.vector.tensor_scalar(out=lam_p[:], in0=lam_e[:], scalar1=-0.25, scalar2=1.0 / 3.0,
                                          op0=ALU.mult, op1=ALU.add))
    DVE.wait(td)
    td = DVE.done(nc.vector.tensor_mul(lam_p[:], lam_p[:], lam_e[:]))
    DVE.wait(td)
    td = DVE.done(nc.vector.tensor_scalar(out=lam_p[:], in0=lam_p[:], scalar1=-1.0, scalar2=0.5,
                                          op0=ALU.mult, op1=ALU.add))
    DVE.wait(td)
    td = DVE.done(nc.vector.tensor_mul(lam_p[:], lam_p[:], lam_e[:]))
    DVE.wait(td)
    td = DVE.done(nc.vector.tensor_scalar(out=lam_p[:], in0=lam_p[:], scalar1=-1.0, scalar2=1.0,
                                          op0=ALU.mult, op1=ALU.add))
    DVE.wait(td)
    td = DVE.done(nc.vector.tensor_mul(lam_p[:], lam_p[:], lam_e[:]))
    DVE.wait(td)
    td = DVE.done(nc.vector.tensor_scalar_mul(coef[:], lam_p[:], -8.0))
    td = DVE.done(nc.vector.tensor_scalar_mul(coef2[:], lam_p[:], -16.0))
    td = DVE.done(nc.vector.tensor_scalar_mul(nbr[:], vecs[:, V_BR:V_BR + 4], -1.0))
    td = DVE.done(nc.vector.tensor_scalar_mul(nbi[:], vecs[:, V_BI:V_BI + 4], -1.0))
    td = DVE.done(nc.vector.memset(vaug[:, :, :, HD:HD + 1], 1.0))
    t_vec_consts = td

    stg_sem = [DmaSem(nc, "stg%d" % i) for i in range(5)]
    stg_free = [None] * 5

    def stage_load(k, src_ap, cols):
        SP.wait(stg_free[k])
        return stg_sem[k].issue(nc.sync.dma_start(
            out=stg[k][:, 0:8 * cols].rearrange("p (k c) -> p k c", k=8),
            in_=src_ap.rearrange("(k p) c -> p k c", p=128)))

    CB = 128
    sidx = 0
    PE.wait(td, t_pool_consts)
    st_setup = dict(sidx=0, nblk=0)
    t_modl = []
    t_gatel = []
    t_w = []

    def task_shift_scale(fcol):
        k = st_setup["sidx"] % 5
        st_setup["sidx"] += 1
        tl = stage_load(k, wmod_d[:, fcol * 128:(fcol + 1) * 128], CB)
        bi_, bk, brel = banks.get()
        PE.wait(tl, brel)
        wv = stg[k][:, 0:8 * CB].rearrange("p (k c) -> p k c", k=8)
        for kc in range(8):
            mm = nc.tensor.matmul(bk[:, 0:NSEQ], wv[:, kc, :], cact[:, kc, :], start=(kc == 0), stop=(kc == 7))
        tp = PE.done(mm)
        stg_free[k] = tp
        DVE.wait(tp)
        fc = fcol % 8
        if fcol < 8:
            tdd = DVE.done(nc.vector.tensor_scalar(out=shT[:, fc, :], in0=bk[:, 0:NSEQ],
                                                   scalar1=vecs[:, V_BSH + fc:V_BSH + fc + 1], scalar2=None,
                                                   op0=ALU.add))
        else:
            tdd = DVE.done(nc.vector.tensor_scalar(out=gsT[:, fc, :], in0=bk[:, 0:NSEQ],
                                                   scalar1=vecs[:, V_BSC + fc:V_BSC + fc + 1], scalar2=1.0,
                                                   op0=ALU.add, op1=ALU.add))
            DVE.wait(tdd)
            tdd = DVE.done(nc.vector.tensor_scalar(out=gsT[:, fc, :], in0=gsT[:, fc, :],
                                                   scalar1=vecs[:, V_NG + fc:V_NG + fc + 1], scalar2=None,
                                                   op0=ALU.mult))
        banks.put(bi_, tdd)
        t_modl.append(tdd)

    def task_gate(gcol):
        k = st_setup["sidx"] % 5
        st_setup["sidx"] += 1
        tl = stage_load(k, wmod_d[:, 2 * D + gcol * 128:2 * D + (gcol + 1) * 128], CB)
        bi_, bk, brel = banks.get()
        PE.wait(tl, brel)
        wv = stg[k][:, 0:8 * CB].rearrange("p (k c) -> p k c", k=8)
        for kc in range(8):
            mm = nc.tensor.matmul(bk[0:NSEQ, 0:CB], cact[:, kc, :], wv[:, kc, :], start=(kc == 0), stop=(kc == 7))
        tp = PE.done(mm)
        stg_free[k] = tp
        DVE.wait(tp, t_small)
        tdd = DVE.done(nc.vector.tensor_add(gate_rows[:, gcol * 128:(gcol + 1) * 128],
                                            gate_rows[:, gcol * 128:(gcol + 1) * 128], bk[0:NSEQ, 0:CB]))
        banks.put(bi_, tdd)
        t_gatel.append(tdd)

    cast_engs = [(ACT, lambda o, i: nc.scalar.copy(o, i)),
                 (ACT, lambda o, i: nc.scalar.copy(o, i)),
                 (POOL, lambda o, i: nc.gpsimd.tensor_copy(o, i))]

    def task_weight(dst, src, cb_):
        k = st_setup["sidx"] % 5
        st_setup["sidx"] += 1
        tl = stage_load(k, src[:, cb_ * CB:(cb_ + 1) * CB], CB)
        E, fn = cast_engs[st_setup["nblk"] % 3]
        st_setup["nblk"] += 1
        E.wait(tl)
        tcst = E.done(fn(dst[:, :, cb_ * CB:(cb_ + 1) * CB],
                         stg[k][:, 0:8 * CB].rearrange("p (k c) -> p k c", k=8)))
        stg_free[k] = tcst
        t_w.append(tcst)

    mod_tasks = [(task_shift_scale, (f_,)) for f_ in range(16)] + [(task_gate, (g_,)) for g_ in range(8)]
    w_tasks = [(task_weight, (win_sb, win_d, c_)) for c_ in range(3 * D // CB)] + \
              [(task_weight, (wout_sb, wout_d, c_)) for c_ in range(D // CB)]
    for n_ in range(max(len(mod_tasks), len(w_tasks))):
        if n_ < len(w_tasks):
            w_tasks[n_][0](*w_tasks[n_][1])
        if n_ < len(mod_tasks):
            mod_tasks[n_][0](*mod_tasks[n_][1])
    t_mod = t_modl
    t_gate_rows = t_gatel
    t_setup = t_w + t_modl + t_gatel + [t_vec_consts, t_pool_consts] + [f for f in stg_free if f]
    for E in (PE, ACT, DVE, POOL, SP):
        E.wait(t_setup)

    xt_sem = [DmaSem(nc, "xt%d" % i) for i in range(NXB)]
    xt_free = [None] * NXB
    xr_sem = [DmaSem(nc, "xr%d" % i) for i in range(2)]
    xr_free = [None] * 2
    cs_sem = [DmaSem(nc, "cs%d" % i) for i in range(2)]
    cs_free = [None] * 2
    out_sem = [DmaSem(nc, "out%d" % i) for i in range(2)]
    st = dict(xt_n=0, xr_n=0, pt_n=0, chunk_n=0, ct_n=0, rot_n=0)
    PT_free = [None] * NPT
    free_tok = {}

    def fr(name):
        return free_tok.get(name)

    x_loaded = {}

    def issue_x_load(s, i):
        toks = []
        for tt in range(2):
            b = st["xt_n"] % NXB
            st["xt_n"] += 1
            SP.wait(xt_free[b])
            T = 2 * i + tt
            toks.append((b, xt_sem[b].issue(LZp.dma_start(out=xt[b][:], in_=x_d[s, T * 128:(T + 1) * 128, :]), SP, 524288)))
        x_loaded[(s, i)] = toks

    cs_loaded = {}

    def issue_cs_load(s, i):
        SP.wait(cs_free[0])
        tcs = cs_sem[0].issue(LZp.dma_start(out=cosb[0][:], in_=cos_d[:, i * BLK:(i + 1) * BLK]), SP, 131072)
        tcs = cs_sem[0].issue(LZp.dma_start(out=sinb[0][:], in_=sin_d[:, i * BLK:(i + 1) * BLK]), SP, 131072)
        cs_loaded[(s, i)] = tcs

    order = [(s, i) for s in range(NSEQ) for i in range(NB)]
    banks.free.remove(7)
    LZt, LZs, LZv, LZg, LZp = LazyEng(PE), LazyEng(ACT), LazyEng(DVE), LazyEng(POOL), LazyEng(SP)
    Sched.ops = []
    Sched.on = True

    class Cx:
        pass

    def emit_gate_bc(s):
        for half in range(2):
            bi_, bk, brel = banks.get()
            PE.wait(brel, t_gate_rows, t_pool_consts)
            tp = PE.done(LZt.matmul(bk[:, :], sel4[:, s, :], gate_rows[:, half * 512:(half + 1) * 512],
                                          start=True, stop=True))
            ACT.wait(tp, fr("gate_bc"))
            tg = ACT.done(LZs.copy(gate_bc[:, half * 512:(half + 1) * 512], bk[:, :]))
            banks.put(bi_, tg)
        free_tok["gate_bc_ready"] = tg

    def ef_A0(cx):
        s, i = cx.s, cx.i
        cx.xtoks = x_loaded.pop((s, i))
        cx.t_cs = cs_loaded.pop((s, i))
        xtoks = cx.xtoks
        if i == 0:
            POOL.wait([fr("xl%d" % c_) for c_ in range(4)])
            th = POOL.done(LZg.memset(xl[:, :, 0:3], 0.0))
            POOL.wait([fr("hc%d" % c_) for c_ in range(4)], [fr("rec%d" % c_) for c_ in range(2)])
            th2 = POOL.done(LZg.memset(hcarry[:], 0.0))
            free_tok["xl_halo_ready"] = th
            free_tok["hcarry_ready"] = th2
        t_xn = []
        for tt in range(2):
            b, tl = xtoks[tt]
            ACT.wait(tl, fr("junk"), fr("stat_in"))
            tq = ACT.done(LZs.activation(out=junk[:], in_=xt[b][:], func=AF.Square,
                                               accum_out=stat[:, tt:tt + 1]))
            free_tok["junk"] = tq
            t_xn.append(tq)
        ACT.wait(t_xn)
        tq = ACT.done(LZs.activation(out=stat[:, 2:4], in_=stat[:, 0:2], func=AF.Ln, scale=1.0 / D, bias=EPS))
        ACT.wait(tq)
        t_rstd = ACT.done(LZs.activation(out=stat[:, 4:6], in_=stat[:, 2:4], func=AF.Exp, scale=-0.5))
        t_xn2 = []
        for tt in range(2):
            b, tl = xtoks[tt]
            DVE.wait(t_rstd, tl)
            t_xn2.append(DVE.done(LZv.tensor_scalar(out=xt[b][:], in0=xt[b][:], scalar1=stat[:, 4 + tt:5 + tt],
                                                          scalar2=None, op0=ALU.mult)))
        free_tok["stat_in"] = t_xn2
        t_hT = []
        t_tr_all = []
        for fp in range(4):
            bi_, bk, brel = banks.get()
            PE.wait(brel, t_xn2)
            for f2 in range(2):
                fc = 2 * fp + f2
                for tt in range(2):
                    b, _ = xtoks[tt]
                    mm = LZt.transpose(bk[:, f2 * 256 + tt * 128:f2 * 256 + (tt + 1) * 128],
                                             xt[b][:, fc * 128:(fc + 1) * 128], ident[:])
            tp = PE.done(mm)
            t_tr_all.append(tp)
            evs = []
            for f2 in range(2):
                fc = 2 * fp + f2
                if fp % 2 == 0:
                    ACT.wait(tp, fr("hT"))
                    evs.append(ACT.done(LZs.activation(out=hT[:, fc, :], in_=bk[:, f2 * 256:(f2 + 1) * 256],
                                                             func=AF.Identity,
                                                             scale=gsT[:, fc, s:s + 1], bias=shT[:, fc, s:s + 1])))
                else:
                    DVE.wait(tp, fr("hT"))
                    evs.append(DVE.done(LZv.tensor_scalar(out=hT[:, fc, :], in0=bk[:, f2 * 256:(f2 + 1) * 256],
                                                                scalar1=gsT[:, fc, s:s + 1], scalar2=shT[:, fc, s:s + 1],
                                                                op0=ALU.mult, op1=ALU.add)))
            banks.put(bi_, evs)
            t_hT.extend(evs)
        for tt in range(2):
            xt_free[xtoks[tt][0]] = list(t_tr_all)
        cx.t_hT = t_hT
        cx.t_hT_rd = []
        cx.t_cs_rd = []
        if cx.oi + 1 < len(order):
            issue_x_load(*order[cx.oi + 1])

    def inproj_fm(col0, bk, half):
        for kc in range(8):
            mm_ = LZt.matmul(bk[:, half * 256:(half + 1) * 256], win_sb[:, kc, col0:col0 + 128], hT[:, kc, :],
                                   start=(kc == 0), stop=(kc == 7))
        return mm_

    def rope_tiles(cx, kind):
        s, i = cx.s, cx.i
        c0 = i * BLK
        colbase = 512 if kind == "k" else 0
        toks = []
        t_kmean = []
        for ht in range(4):
            bi_, bk, brel = banks.get()
            PE.wait(brel, cx.t_hT)
            tp = PE.done(inproj_fm(colbase + ht * 128, bk, 0))
            r = st["rot_n"] % 2
            st["rot_n"] += 1
            DVE.wait(tp, fr("kq32_%d" % r))
            tc = DVE.done(LZv.tensor_copy(kq32[r][:], bk[:, 0:256]))
            PE.wait(tc)
            tr = PE.done(LZt.matmul(bk[:, 256:512], rmat[:], kq32[r][:], start=True, stop=True))
            cx.t_hT_rd.append(tp)
            POOL.wait(tc, cx.t_cs, fr("rt1_%d" % r))
            tm1 = POOL.done(LZg.tensor_mul(rt1[r][:], kq32[r][:], cosb[0][:]))
            DVE.wait(tr, cx.t_cs, fr("rt2_%d" % r))
            tm2 = DVE.done(LZv.tensor_mul(rt2[r][:], bk[:, 256:512], sinb[0][:]))
            cx.t_cs_rd += [tm1, tm2]
            banks.put(bi_, tm2)
            free_tok["kq32_%d" % r] = [tm1, tr]
            DVE.wait(tm1, tm2)
            if kind == "k":
                DVE.wait(fr("krot"))
                t3 = DVE.done(LZv.tensor_add(krot[:], rt1[r][:], rt2[r][:]))
                free_tok["rt1_%d" % r] = t3
                free_tok["rt2_%d" % r] = t3
                ACT.wait(t3, fr("kT"))
                tk = ACT.done(LZs.copy(kT[:, ht, c0:c0 + BLK], krot[:]))
                toks.append(tk)
                DVE.wait(t3, fr("ksum"))
                t4 = DVE.done(LZv.tensor_reduce(out=ksum[:, ht:ht + 1], in_=krot[:], axis=AX.X, op=ALU.add))
                free_tok["krot"] = [tk, t4]
                DVE.wait(t4, fr("kmean"))
                t5 = DVE.done(LZv.tensor_scalar(out=kmean[:, ht, i:i + 1], in0=ksum[:, ht:ht + 1],
                                                      scalar1=1.0 / BLK, scalar2=None, op0=ALU.mult))
                free_tok["ksum"] = t5
                t_kmean.append(t5)
            else:
                DVE.wait(fr("qT"))
                t3 = DVE.done(LZv.tensor_add(qT[:, ht, :], rt1[r][:], rt2[r][:]))
                free_tok["rt1_%d" % r] = t3
                free_tok["rt2_%d" % r] = t3
                toks.append(t3)
        if kind == "k":
            cx.t_kT = toks
            cx.t_kmean = t_kmean
        else:
            cx.t_qT = toks

    def ef_K(cx):
        rope_tiles(cx, "k")

    def ef_V(cx):
        i = cx.i
        t_v = []
        for tt in range(2):
            T = 2 * i + tt
            bi_, bk, brel = banks.get()
            PE.wait(brel, cx.t_hT)
            for kc in range(8):
                mm = LZt.matmul(bk[:, :], hT[:, kc, tt * 128:(tt + 1) * 128], win_sb[:, kc, 1024:1536],
                                      start=(kc == 0), stop=(kc == 7))
            tp = PE.done(mm)
            cx.t_hT_rd.append(tp)
            DVE.wait(tp, fr("vaug"), fr("vaug_chain"))
            tv = DVE.done(LZv.tensor_copy(vaug[:, T, :, 0:HD], bk[:, :].rearrange("p (h d) -> p h d", h=NH)))
            banks.put(bi_, tv)
            free_tok["vaug_chain"] = tv
            t_v.append(tv)
        cx.t_v = t_v

    def ef_L(cx, ci):
        s, i = cx.s, cx.i
        p = ci % 2
        ylT = ylT2[cx.oi % 2]
        if ci == 0:
            cx.t_yp = []
        bi_, bk, brel = banks.get()
        PE.wait(brel, cx.t_hT)
        tpx = PE.done(inproj_fm(2048 + ci * 128, bk, 0))
        PE.wait(tpx, cx.t_hT)
        tpz = PE.done(inproj_fm(2560 + ci * 128, bk, 1))
        cx.t_hT_rd += [tpx, tpz]
        ACT.wait(tpx, tpz, fr("xl%d" % ci), fr("xl_halo_ready"))
        tx = ACT.done(LZs.copy(xl[:, ci, 3:3 + BLK], bk[:, 0:256]))
        ACT.wait(tpz, fr("uu%d" % p))
        tez = ACT.done(LZs.activation(out=uu[p][:], in_=bk[:, 256:512], func=AF.Exp, scale=-1.0))
        ACT.wait(tez)
        t8 = ACT.done(LZs.activation(out=uu[p][:], in_=uu[p][:], func=AF.Ln, bias=1.0))
        ACT.wait(t8)
        t8 = ACT.done(LZs.activation(out=uu[p][:], in_=uu[p][:], func=AF.Exp, scale=-1.0))
        DVE.wait(t8, tpz, fr("szl%d" % ci))
        t_szl = DVE.done(LZv.tensor_mul(szl[:, ci, :], bk[:, 256:512], uu[p][:]))
        banks.put(bi_, [tx, t_szl])
        free_tok["uu%d" % p] = t_szl
        POOL.wait(tx, fr("xc%d" % p), fr("xl%d" % ci), fr("xl_halo_ready"))
        t9 = POOL.done(LZg.tensor_scalar(out=xc[p][:], in0=xl[:, ci, 0:BLK],
                                               scalar1=vecs[:, V_CW + ci * 4:V_CW + ci * 4 + 1],
                                               scalar2=vecs[:, V_CB + ci:V_CB + ci + 1], op0=ALU.mult, op1=ALU.add))
        for w in range(1, 4):
            DVE.wait(t9)
            t9 = DVE.done(LZv.scalar_tensor_tensor(out=xc[p][:], in0=xl[:, ci, w:w + BLK],
                                                         scalar=vecs[:, V_CW + ci * 4 + w:V_CW + ci * 4 + w + 1],
                                                         in1=xc[p][:], op0=ALU.mult, op1=ALU.add))
        POOL.wait(t9, fr("xcb%d" % p))
        t10 = POOL.done(LZg.tensor_copy(xcb[p][:], xc[p][:]))
        POOL.wait(t9, tx)
        t_halo = POOL.done(LZg.tensor_copy(xl[:, ci, 0:3], xl[:, ci, BLK:BLK + 3]))
        free_tok["xl%d" % ci] = t_halo
        bi_, bk, brel = banks.get()
        PE.wait(brel, t10)
        LZt.matmul(bk[:, 0:256], wr_bd[:, ci, :], xcb[p][:], start=True, stop=True)
        tpg = PE.done(LZt.matmul(bk[:, 256:512], wi_bd[:, ci, :], xcb[p][:], start=True, stop=True))
        free_tok["xcb%d" % p] = tpg
        ACT.wait(tpg, fr("er%d" % p), fr("ei%d" % p))
        ter = ACT.done(LZs.activation(out=er[p], in_=bk[:, 0:256], func=AF.Exp, scale=-1.0,
                                            bias=nbr[:, ci:ci + 1]))
        ACT.wait(tpg, fr("ei%d" % p), fr("er%d" % p))
        tei = ACT.done(LZs.activation(out=ei[p], in_=bk[:, 256:512], func=AF.Exp, scale=-1.0,
                                            bias=nbi[:, ci:ci + 1]))
        banks.put(bi_, [ter, tei])
        ACT.wait(ter, tei)
        t11 = ACT.done(LZs.activation(out=eri[p][:, :, :], in_=eri[p][:, :, :], func=AF.Ln, bias=1.0))
        ACT.wait(t11)
        t11 = ACT.done(LZs.activation(out=eri[p][:, :, :], in_=eri[p][:, :, :], func=AF.Exp, scale=-1.0))
        t12 = t11
        ACT.wait(t11, fr("aa%d" % p))
        ta_ = ACT.done(LZs.activation(out=aa[p][:], in_=er[p], func=AF.Exp, scale=coef[:, ci:ci + 1]))
        POOL.wait(ta_, fr("a2_%d" % p))
        ta2 = POOL.done(LZg.tensor_mul(a2[p][:], aa[p][:], aa[p][:]))
        free_tok["er%d" % p] = ta_
        ACT.wait(ta2)
        ta2 = ACT.done(LZs.activation(out=a2[p][:], in_=a2[p][:], func=AF.Ln, scale=-1.0, bias=1.0))
        ACT.wait(ta2)
        tnrm = ACT.done(LZs.activation(out=a2[p][:], in_=a2[p][:], func=AF.Exp, scale=0.5))
        DVE.wait(t12, t9)
        t12 = DVE.done(LZv.tensor_mul(ei[p], ei[p], xc[p][:]))
        free_tok["xc%d" % p] = t12
        DVE.wait(t12, tnrm)
        t13 = DVE.done(LZv.tensor_mul(ei[p], ei[p], a2[p][:]))
        free_tok["a2_%d" % p] = t13
        DVE.wait(t13, ta_, fr("rec%d" % p), fr("hcarry_ready"), fr("hc%d" % ci))
        t14 = DVE.done(LZv.tensor_tensor_scan(out=rec[p][:], data0=aa[p][:], data1=ei[p],
                                                    initial=hcarry[:, ci:ci + 1], op0=ALU.mult, op1=ALU.add))
        free_tok["aa%d" % p] = t14
        free_tok["ei%d" % p] = t14
        DVE.wait(t14)
        t15 = DVE.done(LZv.tensor_copy(hcarry[:, ci:ci + 1], rec[p][:, BLK - 1:BLK]))
        free_tok["hc%d" % ci] = t15
        POOL.wait(t14, fr("sqb%d" % p))
        t16 = POOL.done(LZg.tensor_mul(sqb[p][:], rec[p][:], rec[p][:]))
        DVE.wait(t_szl, t14, fr("yp"), fr("attn"))
        t17 = DVE.done(LZv.scalar_tensor_tensor(out=yp[:, ci, :], in0=rec[p][:],
                                                      scalar=vecs[:, V_GL + ci:V_GL + ci + 1], in1=szl[:, ci, :],
                                                      op0=ALU.mult, op1=ALU.mult))
        free_tok["rec%d" % p] = [t16, t17, t15]
        free_tok["szl%d" % ci] = t17
        cx.t_yp.append(t17)
        if ci == 0:
            cx.sb_ = banks.get()
            PE.wait(cx.sb_[2])
        sbi, sbk, _ = cx.sb_
        PE.wait(t16, cx.t_stats_prev if ci > 0 else None)
        tps = PE.done(LZt.matmul(sbk[:, 0:256], onesm[:], sqb[p][:], start=(ci == 0), stop=(ci == 3)))
        cx.t_stats_prev = tps
        free_tok["sqb%d" % p] = tps
        if ci == 3:
            free_tok["xl_halo"] = t_halo
            free_tok["hcarry"] = t15
            ACT.wait(tps, fr("rstd_bc"))
            tl1 = ACT.done(LZs.activation(out=rstd_bc[:], in_=sbk[:, 0:256], func=AF.Ln, bias=EPS))
            banks.put(sbi, tl1)
            ACT.wait(tl1)
            tl2 = ACT.done(LZs.activation(out=rstd_bc[:], in_=rstd_bc[:], func=AF.Exp, scale=-0.5))
            t_ylT = []
            for c2 in range(4):
                E = POOL if c2 % 2 else DVE
                E.wait(tl2, cx.t_yp[c2], fr("ylT%d" % (cx.oi % 2)))
                fn = LZg.tensor_mul if c2 % 2 else LZv.tensor_mul
                t_ylT.append(E.done(fn(ylT[:, c2, :], yp[:, c2, :], rstd_bc[:])))
            free_tok["rstd_bc"] = t_ylT
            free_tok["yp"] = t_ylT
            cx.t_ylT = t_ylT

    def lf_Q(cx):
        rope_tiles(cx, "q")
        cs_free[0] = list(cx.t_cs_rd)
        if cx.oi + 1 < len(order):
            issue_cs_load(*order[cx.oi + 1])

    def lf_Z(cx):
        t_sza = []
        for tt in range(2):
            bi_, bk, brel = banks.get()
            PE.wait(brel, cx.t_hT)
            for kc in range(8):
                mm = LZt.matmul(bk[:, :], hT[:, kc, tt * 128:(tt + 1) * 128], win_sb[:, kc, 1536:2048],
                                      start=(kc == 0), stop=(kc == 7))
            tp = PE.done(mm)
            cx.t_hT_rd.append(tp)
            ACT.wait(tp, fr("ez%d" % tt))
            te = ACT.done(LZs.activation(out=ez[tt][:], in_=bk[:, :], func=AF.Exp, scale=-1.0))
            ACT.wait(te)
            t6 = ACT.done(LZs.activation(out=ez[tt][:], in_=ez[tt][:], func=AF.Ln, bias=1.0))
            ACT.wait(t6)
            t6 = ACT.done(LZs.activation(out=ez[tt][:], in_=ez[tt][:], func=AF.Exp, scale=-1.0))
            DVE.wait(t6, tp, fr("sza"))
            t7 = DVE.done(LZv.tensor_mul(sza[:, tt, :], bk[:, :], ez[tt][:]))
            free_tok["ez%d" % tt] = t7
            banks.put(bi_, t7)
            t_sza.append(t7)
        cx.t_sza = t_sza
        free_tok["hT"] = list(cx.t_hT_rd)

    def at_SEL(cx):
        i = cx.i
        cx.t_acc = []
        cx.t_PV_all = []
        cx.t_ST_all = []
        cx.t_tcm_all = []
        cx.t_sel = None
        if i < 4:
            return
        gb = [banks.get() for _ in range(2)]
        PE.wait(gb[0][2], gb[1][2], cx.t_qT, cx.t_kmean)
        for tt in range(2):
            for ht in range(4):
                for hh in range(2):
                    hb = hh * 64
                    col = (tt * 4 + ht) * 8
                    mm = LZt.matmul(gb[hh][1][:, col:col + i],
                                          qT[hb:hb + 64, ht, tt * 128:(tt + 1) * 128],
                                          kmean[hb:hb + 64, ht, 0:i], start=True, stop=True)
        tpg = PE.done(mm)
        g5 = g32[:, :, :, :].rearrange("p t (a b) j -> p t a b j", b=2)
        tg32s = []
        for hh in range(2):
            ACT.wait(tpg, fr("g32"))
            tg32 = ACT.done(LZs.copy(
                g5[:, :, :, hh, 0:i],
                gb[hh][1][:, 0:64].rearrange("p (t a j) -> p t a j", t=2, a=4)[:, :, :, 0:i]))
            banks.put(gb[hh][0], tg32)
            tg32s.append(tg32)
        t_sel = []
        for tt in range(2):
            gv = g32[:, tt, :, 0:i]
            DVE.wait(tg32s, fr("cmpb"))
            tc1 = DVE.done(LZv.tensor_tensor(out=cmpb[:, :, 0:i, 0:i],
                                                   in0=gv.unsqueeze(2).to_broadcast((128, NH, i, i)),
                                                   in1=gv.unsqueeze(3).to_broadcast((128, NH, i, i)),
                                                   op=ALU.is_gt))
            DVE.wait(tc1, fr("rank"))
            tc2 = DVE.done(LZv.tensor_reduce(out=rank[:, :, 0:i], in_=cmpb[:, :, 0:i, 0:i], axis=AX.X,
                                                   op=ALU.add))
            free_tok["cmpb"] = tc2
            DVE.wait(tc2, fr("selx"))
            tc3 = DVE.done(LZv.tensor_single_scalar(out=selx[:, tt, :, 0:i], in_=rank[:, :, 0:i], scalar=3.0,
                                                          op=ALU.is_lt))
            free_tok["rank"] = tc3
            t_sel.append(tc3)
        free_tok["g32"] = t_sel
        cx.t_sel = t_sel

    def at_P(cx, hp):
        i = cx.i
        steps = [i] + list(range(i))
        st_state = {}
        st_acc = {}
        accp = acc[:, :, 2 * hp:2 * hp + 2, :]

        def emit_st(j):
            own = (j == i)
            bx = [banks.get() for _ in range(2)]
            PE.wait(bx[0][2], bx[1][2], cx.t_qT, cx.t_kT)
            mm_ = None
            for ktl in range(2):
                kt = 2 * j + ktl
                qlo = 128 if (own and ktl == 1) else 0
                for hh in range(2):
                    hb = hh * 64
                    mm_ = LZt.matmul(bx[hh][1][:, ktl * 256 + qlo:(ktl + 1) * 256],
                                           kT[hb:hb + 64, hp, kt * 128:(kt + 1) * 128],
                                           qT[hb:hb + 64, hp, qlo:256], start=True, stop=True)
            tps_ = PE.done(mm_)
            cx.t_ST_all.append(tps_)
            res_ = []
            for hh in range(2):
                pb = st["pt_n"] % NPT
                st["pt_n"] += 1
                ACT.wait(tps_, PT_free[pb])
                if own:
                    te1_ = ACT.done(LZs.activation(out=PT[pb][:, 0, :], in_=bx[hh][1][:, 0:256], func=AF.Exp,
                                                   scale=0.125))
                    ACT.wait(tps_, PT_free[pb])
                    te_ = ACT.done(LZs.activation(out=PT[pb][:, 1, 128:256], in_=bx[hh][1][:, 384:512],
                                                        func=AF.Exp, scale=0.125))
                else:
                    te_ = ACT.done(LZs.activation(out=PT[pb][:, :, :].rearrange("p a q -> p (a q)"),
                                                        in_=bx[hh][1][:, :], func=AF.Exp, scale=0.125))
                banks.put(bx[hh][0], [te_, te1_] if own else te_)
                tok = te_
                if own:
                    POOL.wait(te1_, t_pool_consts)
                    tm1_ = POOL.done(LZg.tensor_mul(PT[pb][:, 0, 0:128], PT[pb][:, 0, 0:128], tri[:]))
                    POOL.wait(te_, t_pool_consts)
                    tm2_ = POOL.done(LZg.tensor_mul(PT[pb][:, 1, 128:256], PT[pb][:, 1, 128:256], tri[:]))
                    tok = [tm1_, tm2_, te1_, te_]
                res_.append((pb, tok))
            st_state[j] = res_

        def emit_pv(j):
            own = (j == i)
            (pbA, tokA), (pbB, tokB) = st_state.pop(j)
            zi, zk, zrel = banks.get()
            PE.wait(tokA, tokB, cx.t_v, zrel)
            mm_ = None
            for tt in range(2):
                for hh in range(2):
                    h = 2 * hp + hh
                    pb = (pbA, pbB)[hh]
                    g = tt * 2 + hh
                    dst = zk[:, g * 65:(g + 1) * 65]
                    ktls = [0] if (own and tt == 0) else [0, 1]
                    for n_, ktl in enumerate(ktls):
                        mm_ = LZt.matmul(dst, PT[pb][:, ktl, tt * 128:(tt + 1) * 128],
                                               vaug[:, 2 * j + ktl, h, :],
                                               start=(n_ == 0), stop=(n_ == len(ktls) - 1))
            tpv_ = PE.done(mm_)
            cx.t_PV_all.append(tpv_)
            PT_free[pbA] = tpv_
            PT_free[pbB] = tpv_
            zv = zk[:, 0:4 * 65].rearrange("p (t h d) -> p t h d", t=2, h=2)
            if own:
                ACT.wait(tpv_, fr("acc"))
                ta_ = ACT.done(LZs.copy(accp, zv))
                banks.put(zi, ta_)
            elif i <= 3:
                DVE.wait(tpv_, st_acc["tok"])
                ta_ = DVE.done(LZv.tensor_add(accp, accp, zv))
                banks.put(zi, ta_)
            else:
                cb2 = st["ct_n"] % 2
                st["ct_n"] += 1
                DVE.wait(tpv_, cx.t_sel, fr("ctmp%d" % cb2))
                tcm = DVE.done(LZv.tensor_tensor(
                    out=ctmp[cb2][:, :, :, :], in0=zv,
                    in1=selx[:, :, 2 * hp:2 * hp + 2, j].unsqueeze(3).to_broadcast((128, 2, 2, 65)),
                    op=ALU.mult))
                banks.put(zi, tcm)
                cx.t_tcm_all.append(tcm)
                POOL.wait(tcm, st_acc["tok"])
                ta_ = POOL.done(LZg.tensor_add(accp, accp, ctmp[cb2][:, :, :, :]))
                free_tok["ctmp%d" % cb2] = ta_
            st_acc["tok"] = ta_
            return tpv_

        tpv = None
        emit_st(steps[0])
        if len(steps) > 1:
            emit_st(steps[1])
        for n_ in range(len(steps)):
            tpv = emit_pv(steps[n_])
            if n_ + 2 < len(steps):
                emit_st(steps[n_ + 2])
        cx.t_acc.append(st_acc["tok"])
        if hp == 3:
            free_tok["qT"] = list(cx.t_ST_all)
            if i >= 4:
                free_tok["selx"] = list(cx.t_tcm_all)

    def bk_N(cx):
        t_ya = []
        for tt in range(2):
            DVE.wait(cx.t_acc, fr("rden"))
            tn1 = DVE.done(LZv.reciprocal(rden[:, tt, :], acc[:, tt, :, HD]))
            DVE.wait(tn1, fr("attn"), fr("yp"))
            tn2 = DVE.done(LZv.tensor_mul(attn[:, tt, :].rearrange("p (h d) -> p h d", h=NH),
                                                acc[:, tt, :, 0:HD],
                                                rden[:, tt, :].unsqueeze(2).to_broadcast((128, NH, HD))))
            ACT.wait(tn2, fr("junk"), fr("stat_a"))
            tn3 = ACT.done(LZs.activation(out=junk[:, 0:512], in_=attn[:, tt, :], func=AF.Square,
                                                accum_out=stat[:, 6 + tt:7 + tt]))
            free_tok["junk"] = tn3
            ACT.wait(tn3)
            tn4 = ACT.done(LZs.activation(out=stat[:, 8 + tt:9 + tt], in_=stat[:, 6 + tt:7 + tt], func=AF.Ln,
                                                scale=1.0 / 512.0, bias=EPS))
            ACT.wait(tn4)
            tn5 = ACT.done(LZs.activation(out=stat[:, 10 + tt:11 + tt], in_=stat[:, 8 + tt:9 + tt], func=AF.Exp,
                                                scale=-0.5))
            DVE.wait(tn5, cx.t_sza[tt])
            tn6 = DVE.done(LZv.scalar_tensor_tensor(out=attn[:, tt, :], in0=attn[:, tt, :],
                                                          scalar=stat[:, 10 + tt:11 + tt], in1=sza[:, tt, :],
                                                          op0=ALU.mult, op1=ALU.mult))
            t_ya.append(tn6)
        free_tok["acc"] = t_ya
        free_tok["rden"] = t_ya
        free_tok["sza"] = t_ya
        free_tok["stat_a"] = t_ya
        t_yaT = []
        t_tr2_all = []
        for mp in range(2):
            bi_, bk, brel = banks.get()
            PE.wait(brel, t_ya, t_pool_consts)
            for m2 in range(2):
                m = 2 * mp + m2
                for tt in range(2):
                    mm = LZt.transpose(bk[:, m2 * 256 + tt * 128:m2 * 256 + (tt + 1) * 128],
                                             attn[:, tt, m * 128:(m + 1) * 128], ident[:])
            tp = PE.done(mm)
            evs = []
            for m2 in range(2):
                m = 2 * mp + m2
                ACT.wait(tp, fr("yaT"))
                evs.append(ACT.done(LZs.activation(out=yaT[:, m, :], in_=bk[:, m2 * 256:(m2 + 1) * 256],
                                                         func=AF.Identity, scale=vecs[:, V_GA + m:V_GA + m + 1])))
            banks.put(bi_, evs)
            t_yaT.extend(evs)
            t_tr2_all.append(tp)
        free_tok["attn"] = t_tr2_all
        cx.t_yaT = t_yaT

    def bk_O(cx):
        s, i = cx.s, cx.i
        ylT = ylT2[cx.oi % 2]
        t_oproj = []
        t_to1 = []
        for tt in range(2):
            T = 2 * i + tt
            rb = st["xr_n"] % 2
            st["xr_n"] += 1
            SP.wait(xr_free[rb])
            t_xr = xr_sem[rb].issue(LZp.dma_start(out=xr[rb][:], in_=x_d[s, T * 128:(T + 1) * 128, :]), SP, 524288)
            t_res = []
            for half in range(2):
                bi_, bk, brel = banks.get()
                PE.wait(brel, cx.t_yaT, cx.t_ylT)
                for kc in range(8):
                    lhs = yaT[:, kc, tt * 128:(tt + 1) * 128] if kc < 4 else ylT[:, kc - 4, tt * 128:(tt + 1) * 128]
                    mm = LZt.matmul(bk[:, :], lhs, wout_sb[:, kc, half * 512:(half + 1) * 512],
                                          start=(kc == 0), stop=(kc == 7))
                tp = PE.done(mm)
                t_oproj.append(tp)
                ob2 = half
                DVE.wait(tp, fr("ez%d" % ob2), fr("gate_bc_ready"))
                to1 = DVE.done(LZv.tensor_mul(otmp[ob2][:], bk[:, :], gate_bc[:, half * 512:(half + 1) * 512]))
                banks.put(bi_, to1)
                t_to1.append(to1)
                DVE.wait(to1, t_xr)
                to2 = DVE.done(LZv.tensor_add(xr[rb][:, half * 512:(half + 1) * 512],
                                              xr[rb][:, half * 512:(half + 1) * 512], otmp[ob2][:]))
                free_tok["ez%d" % ob2] = to2
                t_res.append(to2)
            ACT.wait(t_res, fr("junk"), fr("stat_f"))
            tf1 = ACT.done(LZs.activation(out=junk[:], in_=xr[rb][:], func=AF.Square,
                                                accum_out=stat[:, 12:13]))
            free_tok["junk"] = tf1
            ACT.wait(tf1)
            tf2 = ACT.done(LZs.activation(out=stat[:, 13:14], in_=stat[:, 12:13], func=AF.Ln, scale=1.0 / D, bias=EPS))
            ACT.wait(tf2)
            tf3 = ACT.done(LZs.activation(out=stat[:, 14:15], in_=stat[:, 13:14], func=AF.Exp, scale=-0.5))
            DVE.wait(tf3, t_res)
            tf4 = DVE.done(LZv.scalar_tensor_tensor(out=xr[rb][:], in0=xr[rb][:], scalar=stat[:, 14:15],
                                                          in1=fgain_bc[:], op0=ALU.mult, op1=ALU.mult))
            free_tok["stat_f"] = tf4
            SP.wait(tf4)
            t_out = out_sem[rb].issue(LZp.dma_start(out=out_d[s, T * 128:(T + 1) * 128, :], in_=xr[rb][:]), SP, 524288)
            xr_free[rb] = t_out
        free_tok["yaT"] = list(t_oproj)
        free_tok["ylT%d" % (cx.oi % 2)] = list(t_oproj)
        if i == NB - 1:
            free_tok["gate_bc"] = [tf4] + t_to1
            if s + 1 < NSEQ:
                emit_gate_bc(s + 1)

    def make_cx(oi):
        cx = Cx()
        cx.oi = oi
        cx.s, cx.i = order[oi]
        return cx

    def ef_pieces(cx):
        return [lambda: ef_A0(cx), lambda: ef_K(cx), lambda: ef_V(cx), lambda: ef_L(cx, 0), lambda: ef_L(cx, 1),
                lambda: ef_L(cx, 2), lambda: ef_L(cx, 3)]

    def at_pieces(cx):
        return [lambda: at_SEL(cx), lambda: at_P(cx, 0), lambda: at_P(cx, 1), lambda: at_P(cx, 2), lambda: at_P(cx, 3)]

    emit_gate_bc(0)
    issue_x_load(*order[0])
    issue_cs_load(*order[0])
    cur = make_cx(0)
    for f in ef_pieces(cur):
        f()
    lf_Q(cur)
    lf_Z(cur)
    for oi in range(len(order)):
        nxt = make_cx(oi + 1) if oi + 1 < len(order) else None
        A = at_pieces(cur)
        if nxt is None:
            for f in A:
                f()
        elif nxt.i == 0:
            for f in A:
                f()
            free_tok["kT"] = list(cur.t_ST_all)
            free_tok["vaug"] = list(cur.t_PV_all)
            free_tok["kmean"] = list(cur.t_ST_all) + list(cur.t_PV_all)
            for f in ef_pieces(nxt):
                f()
        else:
            E = ef_pieces(nxt)
            for f in (E[0], A[0], A[1], E[1], A[2], E[2], E[3], A[3], E[4], E[5], A[4], E[6]):
                f()
        bk_N(cur)
        if nxt is not None:
            lf_Q(nxt)
            lf_Z(nxt)
        bk_O(cur)
        cur = nxt

    for E_ in (PE, ACT, DVE, POOL, SP):
        assert not E_.p_calls and not E_.p_waits, E_.name
    scratch_bank = banks.t[7]

    def _pe_fill():
        nc.tensor.matmul(scratch_bank[:, :], win_sb[:, 0, 0:128], win_sb[:, 1, 0:512], start=True, stop=True)

    span_est = run_schedule((PE, ACT, DVE, POOL, SP), pe_fill=_pe_fill if PE_FILL else None)
    nc.sched_span_us = span_est
    for o_ in out_sem:
        SP.wait((o_.sem, o_.n, o_.key))
    return nc


_CACHE = {}


def _consts(S):
    pos = np.arange(S, dtype=np.float32)
    inv_freq = (np.float32(10000.0) ** (-np.arange(0, HD, 2, dtype=np.float32) / np.float32(HD))).astype(np.float32)
    ang = (pos[:, None] * inv_freq[None, :]).astype(np.float32)
    cos = np.cos(ang).astype(np.float32).T
    sin = np.sin(ang).astype(np.float32).T
    cosT = np.tile(cos, (4, 1))
    sinS = np.concatenate([-sin, sin, -sin, sin], axis=0)
    ident = np.eye(128, dtype=np.float32)
    rmat = np.zeros((128, 128), np.float32)
    for d in range(128):
        partner = (d // 64) * 64 + ((d % 64) + 32) % 64
        rmat[partner, d] = 1.0
    tri = np.triu(np.ones((128, 128), np.float32))
    return dict(cosT=np.ascontiguousarray(cosT), sinS=np.ascontiguousarray(sinS), ident=ident, rmat=rmat, tri=tri)


def _sel4(nseq):
    m = np.zeros((nseq, nseq, 128), np.float32)
    for b in range(nseq):
        m[b, b, :] = 1.0
    return m.reshape(nseq, nseq * 128)


def _fm(v, nchunk):
    return np.ascontiguousarray(np.asarray(v, np.float32).reshape(nchunk, 128).T)


def make_in_maps(inputs, n_cores, nseq, S):
    x = np.asarray(inputs["x"], np.float32)
    c = np.asarray(inputs["c"], np.float32)
    w_mod = np.ascontiguousarray(np.asarray(inputs["w_mod"], np.float32)[0])
    b_mod = np.asarray(inputs["b_mod"], np.float32)[0]
    vec = np.zeros((128, NVEC), np.float32)
    vec[:, 0:8] = _fm(inputs["norm_gain"][0], 8)
    vec[:, 8:16] = _fm(b_mod[0:D], 8)
    vec[:, 16:24] = _fm(b_mod[D:2 * D], 8)
    cw = np.asarray(inputs["conv_w"], np.float32)[0]
    vec[:, 24:40] = np.ascontiguousarray(cw.reshape(4, 4, 128).transpose(2, 1, 0)).reshape(128, 16)
    vec[:, 40:44] = _fm(inputs["conv_b"][0], 4)
    vec[:, 44:48] = _fm(inputs["lru_lambda"][0], 4)
    vec[:, 48:52] = _fm(np.asarray(inputs["b_rgate"], np.float32)[0].reshape(-1), 4)
    vec[:, 52:56] = _fm(np.asarray(inputs["b_igate"], np.float32)[0].reshape(-1), 4)
    vec[:, 56:60] = _fm(inputs["lru_out_gain"][0], 4)
    vec[:, 60:64] = _fm(inputs["attn_out_gain"][0], 4)
    common = dict(
        w_mod=w_mod,
        bgate=np.ascontiguousarray(b_mod[None, 2 * D:3 * D]),
        w_in=np.ascontiguousarray(np.asarray(inputs["w_in"], np.float32)[0]),
        w_out=np.ascontiguousarray(np.asarray(inputs["w_out"], np.float32)[0]),
        w_r=np.ascontiguousarray(np.asarray(inputs["w_rgate"], np.float32)[0]),
        w_i=np.ascontiguousarray(np.asarray(inputs["w_igate"], np.float32)[0]),
        vecs=vec,
        fgain=np.ascontiguousarray(np.asarray(inputs["final_gain"], np.float32)[None, :]),
    )
    common.update(_consts(S))
    common["sel4"] = _sel4(nseq)
    maps = []
    for k in range(n_cores):
        xs = np.ascontiguousarray(x[k * nseq:(k + 1) * nseq, :S])
        cs = c[k * nseq:(k + 1) * nseq]
        cT = np.ascontiguousarray(cs.reshape(nseq, 8, 128).transpose(2, 1, 0))
        m = dict(common)
        m["x"] = xs
        m["cT"] = cT
        maps.append(m)
    return maps


def kernel(**inputs):
    key = (NSEQ_FULL, S_FULL)
    if key not in _CACHE:
        _CACHE[key] = build_program(NSEQ_FULL, S_FULL)
    nc = _CACHE[key]
    maps = make_in_maps(inputs, N_CORES, NSEQ_FULL, S_FULL)
    res = run_bass_kernel_spmd(nc, maps, core_ids=list(range(N_CORES)))
    out = np.concatenate([np.asarray(r["out"], np.float32) for r in res.results], axis=0)
    return out
```

```python
import numpy as np
import concourse.bass as bass
import concourse.mybir as mybir
from concourse.bass_utils import run_bass_kernel_spmd

F32 = mybir.dt.float32
BF16 = mybir.dt.bfloat16
AF = mybir.ActivationFunctionType
ALU = mybir.AluOpType
AX = mybir.AxisListType

N_CORES = 8
D = 1024
S_FULL = 2048
NSEQ_FULL = 4
BLK = 256
NH = 8
HD = 64
EPS = 1e-6
NVEC = 64
PE_FILL = True
CP_W = 0.01


class Op:
    __slots__ = ("eng", "deps", "calls", "dur", "idx", "seq", "start", "end", "dma", "lat")


class Sched:
    ops = []
    on = False


class Eng:
    def __init__(self, nc, eng, name):
        self.nc = nc
        self.e = eng
        self.name = name
        self.sem = nc.alloc_semaphore("ms_" + name)
        self.n = 0
        self.seen = {}
        self.p_waits = []
        self.p_calls = []

    def wait(self, *toks):
        if Sched.on:
            self.p_waits.extend(_flat(toks))
            return
        for t in _flat(toks):
            self.raw_wait(_resolve(t))

    def raw_wait(self, t):
        sem, val, key = t
        if self.seen.get(key, 0) >= val:
            return
        self.e.wait_ge(sem, val)
        self.seen[key] = val

    def done(self, inst):
        if Sched.on:
            assert self.p_calls and self.p_calls[-1] is inst
            op = Op()
            op.eng, op.deps, op.calls = self, self.p_waits, self.p_calls
            op.dma = None
            op.dur = _est_dur(self.name, op.calls)
            op.lat = 0.0
            op.idx = len(Sched.ops)
            op.seq = None
            self.p_waits, self.p_calls = [], []
            Sched.ops.append(op)
            return op
        self.n += 1
        inst.then_inc(self.sem, 1)
        return (self.sem, self.n, self.name)


class LazyEng:
    def __init__(self, E):
        self._E = E

    def __getattr__(self, name):
        E = self._E

        def f(*a, **k):
            call = (name, a, k)
            E.p_calls.append(call)
            return call
        return f


def _resolve(t):
    if isinstance(t, Op):
        assert t.seq is not None, "dependency emitted after its consumer"
        if t.dma is not None:
            return (t.dma.sem, t.seq, t.dma.key)
        return (t.eng.sem, t.seq, t.eng.name)
    return t


def _ap_free(ap):
    n = 1
    for d in list(ap.shape)[1:]:
        n *= int(d)
    return n


def _est_dur(eng, calls):
    t = 0.0
    for (name, a, k) in calls:
        out = k.get("out", a[0] if a else None)
        n = _ap_free(out) if out is not None and hasattr(out, "shape") else 64
        if eng == "pe":
            if name == "transpose":
                t += 0.21
            else:
                lhs = k.get("lhsT", a[1] if len(a) > 1 else None)
                f32 = lhs is not None and lhs.dtype == F32
                rows = int(lhs.shape[0]) if lhs is not None else 128
                c = n / 2400.0
                if f32:
                    c *= 3.7
                elif rows <= 64:
                    c *= 0.56
                t += max(c, 0.035)
        elif eng == "act":
            t += 0.24 + n * 0.0005
        elif eng == "dve":
            t += 0.07 + n * 0.00105
        elif eng == "pool":
            t += 0.12 + n * 0.0021
        else:
            t += 0.1
    return t


def run_schedule(engs, pe_fill=None):
    ops = Sched.ops
    Sched.on = False
    per = {}
    for op in ops:
        per.setdefault(op.eng.name, []).append(op)
    tail = [0.0] * len(ops)
    for op in reversed(ops):
        t_ = tail[op.idx] + (op.lat if op.dma is not None else op.dur)
        tail[op.idx] = t_
        for d in op.deps:
            if isinstance(d, Op) and tail[d.idx] < t_ + 0.2:
                tail[d.idx] = t_ + 0.2 - 0.0
    pos = {e: 0 for e in per}
    free_t = {e: 0.0 for e in per}
    done_flag = [False] * len(ops)
    WIN = 64
    order = []
    remaining = len(ops)
    while remaining:
        best = None
        best_t = None
        best_key = None
        for e, lst in per.items():
            p = pos[e]
            while p < len(lst) and done_flag[lst[p].idx]:
                p += 1
            pos[e] = p
            win = 1 if e == "sp" else WIN
            cnt = 0
            q = p
            while q < len(lst) and cnt < win:
                op = lst[q]
                q += 1
                if done_flag[op.idx]:
                    continue
                cnt += 1
                ok = True
                t = free_t[e]
                for d in op.deps:
                    if isinstance(d, Op):
                        if not done_flag[d.idx]:
                            ok = False
                            break
                        te = d.end + (0.03 if d.eng is op.eng else 0.2)
                        if te > t:
                            t = te
                if not ok:
                    continue
                key = t - CP_W * tail[op.idx]
                if best is None or key < best_key - 1e-9 or (abs(key - best_key) <= 1e-9 and op.idx < best.idx):
                    best, best_t, best_key = op, t, key
        assert best is not None, "scheduler deadlock"
        best.start = best_t
        if best.dma is not None:
            best.end = best_t + best.lat
            free_t[best.eng.name] = best_t + 0.08
        else:
            best.end = best_t + best.dur
            free_t[best.eng.name] = best.end
        done_flag[best.idx] = True
        order.append(best)
        remaining -= 1
    order.sort(key=lambda o: (o.start, o.idx))
    pe_free = 0.0
    n_fill = 0
    for op in order:
        E = op.eng
        if E.name == "pe":
            gap = op.start - pe_free
            if pe_fill is not None and gap > 0.3:
                for _ in range(min(int((gap - 0.08) / 0.215), 80)):
                    pe_fill()
                    n_fill += 1
            pe_free = op.end
        for d in op.deps:
            E.raw_wait(_resolve(d))
        inst = None
        for (name, a, k) in op.calls:
            inst = getattr(E.e, name)(*a, **k)
        if op.dma is not None:
            inst.then_inc(op.dma.sem, 16)
        else:
            E.n += 1
            inst.then_inc(E.sem, 1)
            op.seq = E.n
    span = max(o.end for o in order) if order else 0.0
    Sched.ops = []
    return span


def _flat(toks):
    out = []
    for t in toks:
        if t is None:
            continue
        if isinstance(t, Op):
            out.append(t)
        elif isinstance(t, (list, tuple)) and len(t) == 3 and isinstance(t[2], str):
            out.append(t)
        elif isinstance(t, (list, tuple)):
            out.extend(_flat(t))
        else:
            raise TypeError(t)
    return out


class DmaSem:
    _cnt = 0

    def __init__(self, nc, name):
        self.sem = nc.alloc_semaphore("dma_" + name)
        self.n = 0
        DmaSem._cnt += 1
        self.key = "dma_%s_%d" % (name, DmaSem._cnt)

    def issue(self, inst, eng=None, nbytes=0):
        self.n += 16
        if Sched.on:
            assert eng.p_calls and eng.p_calls[-1] is inst
            op = Op()
            op.eng, op.deps, op.calls = eng, eng.p_waits, eng.p_calls
            op.dma = self
            op.seq = self.n
            op.dur = 0.08
            op.lat = 2.2 + nbytes / 150e3
            op.idx = len(Sched.ops)
            eng.p_waits, eng.p_calls = [], []
            Sched.ops.append(op)
            return op
        inst.then_inc(self.sem, 16)
        return (self.sem, self.n, self.key)


class Banks:
    def __init__(self, nc):
        self.t = [nc.alloc_psum_tensor("bank%d" % i, [128, 512], F32) for i in range(8)]
        self.free = list(range(8))
        self.rel = {i: None for i in range(8)}

    def get(self):
        assert self.free, "out of PSUM banks"
        i = self.free.pop(0)
        return i, self.t[i], self.rel[i]

    def put(self, i, *toks):
        self.rel[i] = _flat(toks)
        self.free.append(i)


def build_program(NSEQ=NSEQ_FULL, S=S_FULL, stop_after=None):
    NB = S // BLK
    NT = S // 128
    nc = bass.Bass("TRN2", target_bir_lowering=False)

    def din(name, shape, dt=F32):
        return nc.dram_tensor(name, list(shape), dt, kind="ExternalInput").ap()

    x_d = din("x", [NSEQ, S, D])
    cT_d = din("cT", [128, 8, NSEQ])
    wmod_d = din("w_mod", [D, 3 * D])
    bgate_d = din("bgate", [1, D])
    win_d = din("w_in", [D, 3 * D])
    wout_d = din("w_out", [D, D])
    wr_d = din("w_r", [8, 64, 64])
    wi_d = din("w_i", [8, 64, 64])
    vec_d = din("vecs", [128, NVEC])
    fg_d = din("fgain", [1, D])
    cos_d = din("cosT", [128, S])
    sin_d = din("sinS", [128, S])
    id_d = din("ident", [128, 128])
    rm_d = din("rmat", [128, 128])
    tri_d = din("tri", [128, 128])
    sel4_d = din("sel4", [NSEQ, NSEQ * 128])
    out_d = nc.dram_tensor("out", [NSEQ, S, D], F32, kind="ExternalOutput").ap()

    PE = Eng(nc, nc.tensor, "pe")
    ACT = Eng(nc, nc.scalar, "act")
    DVE = Eng(nc, nc.vector, "dve")
    POOL = Eng(nc, nc.gpsimd, "pool")
    SP = Eng(nc, nc.sync, "sp")
    banks = Banks(nc)

    def sb(name, shape, dt=F32):
        return nc.alloc_sbuf_tensor("s_" + name, list(shape), dt)

    win_sb = sb("win_sb", [128, 8, 3 * D], BF16)
    wout_sb = sb("wout_sb", [128, 8, D], BF16)
    kT = sb("kT", [128, 4, S], BF16)
    vaug = sb("vaug", [128, NT, NH, HD + 1], BF16)
    ident = sb("ident", [128, 128])
    rmat = sb("rmat", [128, 128])
    onesm = sb("onesm", [128, 128])
    tri = sb("tri", [128, 128], BF16)
    tri32 = sb("tri32", [128, 128])
    wr_bd = sb("wr_bd", [128, 4, 128], BF16)
    wi_bd = sb("wi_bd", [128, 4, 128], BF16)
    vecs = sb("vecs", [128, NVEC])
    gsT = sb("gsT", [128, 8, NSEQ])
    shT = sb("shT", [128, 8, NSEQ])
    coef = sb("coef", [128, 4])
    coef2 = sb("coef2", [128, 4])
    nbr = sb("nbr", [128, 4])
    nbi = sb("nbi", [128, 4])
    gate_rows = sb("gate_rows", [NSEQ, D])
    sel4 = sb("sel4", [NSEQ, NSEQ, 128])
    gate_bc = sb("gate_bc", [128, D])
    fgain_bc = sb("fgain_bc", [128, D])
    kmean = sb("kmean", [128, 4, 8], BF16)
    hcarry = sb("hcarry", [128, 4])
    NXB = 2
    xt = [sb("xt%d" % i, [128, D]) for i in range(NXB)]
    xr = [sb("xr%d" % i, [128, D]) for i in range(2)]
    junk = sb("junk", [128, D], BF16)
    stat = sb("stat", [128, 16])
    hT = sb("hT", [128, 8, BLK], BF16)
    kq32 = [sb("kq32_%d" % i, [128, BLK]) for i in range(2)]
    rt1 = [sb("rt1_%d" % i, [128, BLK]) for i in range(2)]
    rt2 = [sb("rt2_%d" % i, [128, BLK]) for i in range(2)]
    krot = sb("krot", [128, BLK])
    ksum = sb("ksum", [128, 4])
    qT = sb("qT", [128, 4, BLK], BF16)
    cosb = [sb("cosb0", [128, BLK])] * 2
    sinb = [sb("sinb0", [128, BLK])] * 2
    ez = [sb("ez%d" % i, [128, 512]) for i in range(2)]
    sza = sb("sza", [128, 2, 512])
    xl = sb("xl", [128, 4, BLK + 3])
    xc = [sb("xc%d" % i, [128, BLK]) for i in range(2)]
    xcb = [sb("xcb%d" % i, [128, BLK], BF16) for i in range(2)]
    eri = [sb("eri%d" % i, [128, 2, BLK]) for i in range(2)]
    er = [eri[i][:, 0, :] for i in range(2)]
    ei = [eri[i][:, 1, :] for i in range(2)]
    aa = [sb("aa%d" % i, [128, BLK]) for i in range(2)]
    a2 = [sb("a2_%d" % i, [128, BLK]) for i in range(2)]
    uu = [sb("uu%d" % i, [128, BLK]) for i in range(2)]
    rec = [sb("rec%d" % i, [128, BLK]) for i in range(2)]
    sqb = [sb("sqb%d" % i, [128, BLK]) for i in range(2)]
    rstd_bc = sb("rstd_bc", [128, BLK])
    ylT2 = [sb("ylT%d" % i, [128, 4, BLK], BF16) for i in range(2)]
    NPT = 4
    PT = [sb("PT%d" % i, [128, 2, BLK], BF16) for i in range(NPT)]
    accb = sb("accb", [128, 2 * NH * (HD + 1)])
    acc = accb[:, :].rearrange("p (t h d) -> p t h d", t=2, h=NH)
    szl = sb("szl", [128, 4, BLK])
    ctmp = [sb("ctmp%d" % i, [128, 2, 2, HD + 1]) for i in range(2)]
    rden = sb("rden", [128, 2, NH])
    ypat = sb("ypat", [128, 1024])
    attn = ypat[:, :].rearrange("p (t f) -> p t f", t=2)
    yp = ypat[:, :].rearrange("p (c t) -> p c t", c=4)
    yaT = sb("yaT", [128, 4, BLK], BF16)
    g32 = sb("g32", [128, 2, NH, 8])
    cmpb = sb("cmpb", [128, NH, 8, 8])
    rank = sb("rank", [128, NH, 8])
    selx = sb("selx", [128, 2, NH, 8])
    otmp = ez

    V_NG, V_BSH, V_BSC, V_CW, V_CB, V_LAM, V_BR, V_BI, V_GL, V_GA = 0, 8, 16, 24, 40, 44, 48, 52, 56, 60

    stg = [xt[0], xt[1], xr[0], xr[1], ypat]
    ld = DmaSem(nc, "setup")
    t_small = []
    for (dst, src) in ((ident, id_d), (rmat, rm_d), (tri32, tri_d), (vecs, vec_d)):
        t_small.append(ld.issue(nc.sync.dma_start(out=dst[:], in_=src)))
    cTs = sb("cTs", [128, 8, NSEQ])
    t_small.append(ld.issue(nc.sync.dma_start(out=cTs[:], in_=cT_d)))
    t_small.append(ld.issue(nc.sync.dma_start(out=fgain_bc[:], in_=fg_d.to_broadcast((128, D)))))
    t_small.append(ld.issue(nc.sync.dma_start(out=gate_rows[:], in_=bgate_d.to_broadcast((NSEQ, D)))))
    t_small.append(ld.issue(nc.sync.dma_start(out=sel4[:].rearrange("p a b -> p (a b)"), in_=sel4_d)))
    t_small = t_small[-1]

    wbd32 = sza[:, :, :].rearrange("p a (c m) -> p a c m", c=4)
    t0 = POOL.done(nc.gpsimd.memset(wbd32, 0.0))
    SP.wait(t0)
    ld2 = DmaSem(nc, "setup2")
    tw = None
    for which, src in ((0, wr_d), (1, wi_d)):
        for g in range(8):
            ci, g2 = g // 2, g % 2
            tw = ld2.issue(nc.sync.dma_start(
                out=wbd32[g2 * 64:(g2 + 1) * 64, which, ci, g2 * 64:(g2 + 1) * 64], in_=src[g]))
    POOL.wait(tw, t_small)
    t1 = POOL.done(nc.gpsimd.tensor_copy(wr_bd[:], wbd32[:, 0]))
    t1 = POOL.done(nc.gpsimd.tensor_copy(wi_bd[:], wbd32[:, 1]))
    t1 = POOL.done(nc.gpsimd.tensor_copy(tri[:], tri32[:]))
    t1 = POOL.done(nc.gpsimd.memset(onesm[:], 1.0 / 512.0))
    t_pool_consts = t1

    cact = sb("cact", [128, 8, NSEQ])
    ctmp_s = sb("ctmp_s", [128, 8, NSEQ])
    lam_e = sb("lam_e", [128, 4])
    lam_p = sb("lam_p", [128, 4])
    ACT.wait(t_small)
    ta = ACT.done(nc.scalar.activation(out=ctmp_s[:], in_=cTs[:], func=AF.Exp, scale=-1.0))
    tb = ACT.done(nc.scalar.activation(out=lam_e[:], in_=vecs[:, V_LAM:V_LAM + 4], func=AF.Exp, scale=-1.0))
    DVE.wait(ta, tb, t_small)
    td = DVE.done(nc.vector.tensor_scalar_add(ctmp_s[:], ctmp_s[:], 1.0))
    DVE.wait(td)
    td = DVE.done(nc.vector.reciprocal(ctmp_s[:], ctmp_s[:]))
    DVE.wait(td)
    td = DVE.done(nc.vector.tensor_mul(cact[:], cTs[:], ctmp_s[:]))
    td = DVE.done(nc.vector.tensor_scalar(out=lam_p[:], in0=lam_e[:], scalar1=-0.25, scalar2=1.0 / 3.0,
                                          op0=ALU.mult, op1=ALU.add))
    DVE.wait(td)
    td = DVE.done(nc.vector.tensor_mul(lam_p[:], lam_p[:], lam_e[:]))
    DVE.wait(td)
    td = DVE.done(nc.vector.tensor_scalar(out=lam_p[:], in0=lam_p[:], scalar1=-1.0, scalar2=0.5,
                                          op0=ALU.mult, op1=ALU.add))
    DVE.wait(td)
    td = DVE.done(nc.vector.tensor_mul(lam_p[:], lam_p[:], lam_e[:]))
    DVE.wait(td)
    td = DVE.done(nc.vector.tensor_scalar(out=lam_p[:], in0=lam_p[:], scalar1=-1.0, scalar2=1.0,
                                          op0=ALU.mult, op1=ALU.add))
    DVE.wait(td)
    td = DVE.done(nc.vector.tensor_mul(lam_p[:], lam_p[:], lam_e[:]))
    DVE.wait(td)
    td = DVE.done(nc.vector.tensor_scalar_mul(coef[:], lam_p[:], -8.0))
    td = DVE.done(nc.vector.tensor_scalar_mul(coef2[:], lam_p[:], -16.0))
    td = DVE.done(nc.vector.tensor_scalar_mul(nbr[:], vecs[:, V_BR:V_BR + 4], -1.0))
    td = DVE.done(nc.vector.tensor_scalar_mul(nbi[:], vecs[:, V_BI:V_BI + 4], -1.0))
    td = DVE.done(nc.vector.memset(vaug[:, :, :, HD:HD + 1], 1.0))
    t_vec_consts = td

    stg_sem = [DmaSem(nc, "stg%d" % i) for i in range(5)]
    stg_free = [None] * 5

    def stage_load(k, src_ap, cols):
        SP.wait(stg_free[k])
        return stg_sem[k].issue(nc.sync.dma_start(
            out=stg[k][:, 0:8 * cols].rearrange("p (k c) -> p k c", k=8),
            in_=src_ap.rearrange("(k p) c -> p k c", p=128)))

    CB = 128
    sidx = 0
    PE.wait(td, t_pool_consts)
    st_setup = dict(sidx=0, nblk=0)
    t_modl = []
    t_gatel = []
    t_w = []

    def task_shift_scale(fcol):
        k = st_setup["sidx"] % 5
        st_setup["sidx"] += 1
        tl = stage_load(k, wmod_d[:, fcol * 128:(fcol + 1) * 128], CB)
        bi_, bk, brel = banks.get()
        PE.wait(tl, brel)
        wv = stg[k][:, 0:8 * CB].rearrange("p (k c) -> p k c", k=8)
        for kc in range(8):
            mm = nc.tensor.matmul(bk[:, 0:NSEQ], wv[:, kc, :], cact[:, kc, :], start=(kc == 0), stop=(kc == 7))
        tp = PE.done(mm)
        stg_free[k] = tp
        DVE.wait(tp)
        fc = fcol % 8
        if fcol < 8:
            tdd = DVE.done(nc.vector.tensor_scalar(out=shT[:, fc, :], in0=bk[:, 0:NSEQ],
                                                   scalar1=vecs[:, V_BSH + fc:V_BSH + fc + 1], scalar2=None,
                                                   op0=ALU.add))
        else:
            tdd = DVE.done(nc.vector.tensor_scalar(out=gsT[:, fc, :], in0=bk[:, 0:NSEQ],
                                                   scalar1=vecs[:, V_BSC + fc:V_BSC + fc + 1], scalar2=1.0,
                                                   op0=ALU.add, op1=ALU.add))
            DVE.wait(tdd)
            tdd = DVE.done(nc.vector.tensor_scalar(out=gsT[:, fc, :], in0=gsT[:, fc, :],
                                                   scalar1=vecs[:, V_NG + fc:V_NG + fc + 1], scalar2=None,
                                                   op0=ALU.mult))
        banks.put(bi_, tdd)
        t_modl.append(tdd)

    def task_gate(gcol):
        k = st_setup["sidx"] % 5
        st_setup["sidx"] += 1
        tl = stage_load(k, wmod_d[:, 2 * D + gcol * 128:2 * D + (gcol + 1) * 128], CB)
        bi_, bk, brel = banks.get()
        PE.wait(tl, brel)
        wv = stg[k][:, 0:8 * CB].rearrange("p (k c) -> p k c", k=8)
        for kc in range(8):
            mm = nc.tensor.matmul(bk[0:NSEQ, 0:CB], cact[:, kc, :], wv[:, kc, :], start=(kc == 0), stop=(kc == 7))
        tp = PE.done(mm)
        stg_free[k] = tp
        DVE.wait(tp, t_small)
        tdd = DVE.done(nc.vector.tensor_add(gate_rows[:, gcol * 128:(gcol + 1) * 128],
                                            gate_rows[:, gcol * 128:(gcol + 1) * 128], bk[0:NSEQ, 0:CB]))
        banks.put(bi_, tdd)
        t_gatel.append(tdd)

    cast_engs = [(ACT, lambda o, i: nc.scalar.copy(o, i)),
                 (ACT, lambda o, i: nc.scalar.copy(o, i)),
                 (POOL, lambda o, i: nc.gpsimd.tensor_copy(o, i))]

    def task_weight(dst, src, cb_):
        k = st_setup["sidx"] % 5
        st_setup["sidx"] += 1
        tl = stage_load(k, src[:, cb_ * CB:(cb_ + 1) * CB], CB)
        E, fn = cast_engs[st_setup["nblk"] % 3]
        st_setup["nblk"] += 1
        E.wait(tl)
        tcst = E.done(fn(dst[:, :, cb_ * CB:(cb_ + 1) * CB],
                         stg[k][:, 0:8 * CB].rearrange("p (k c) -> p k c", k=8)))
        stg_free[k] = tcst
        t_w.append(tcst)

    mod_tasks = [(task_shift_scale, (f_,)) for f_ in range(16)] + [(task_gate, (g_,)) for g_ in range(8)]
    w_tasks = [(task_weight, (win_sb, win_d, c_)) for c_ in range(3 * D // CB)] + \
              [(task_weight, (wout_sb, wout_d, c_)) for c_ in range(D // CB)]
    for n_ in range(max(len(mod_tasks), len(w_tasks))):
        if n_ < len(w_tasks):
            w_tasks[n_][0](*w_tasks[n_][1])
        if n_ < len(mod_tasks):
            mod_tasks[n_][0](*mod_tasks[n_][1])
    t_mod = t_modl
    t_gate_rows = t_gatel
    t_setup = t_w + t_modl + t_gatel + [t_vec_consts, t_pool_consts] + [f for f in stg_free if f]
    for E in (PE, ACT, DVE, POOL, SP):
        E.wait(t_setup)

    xt_sem = [DmaSem(nc, "xt%d" % i) for i in range(NXB)]
    xt_free = [None] * NXB
    xr_sem = [DmaSem(nc, "xr%d" % i) for i in range(2)]
    xr_free = [None] * 2
    cs_sem = [DmaSem(nc, "cs%d" % i) for i in range(2)]
    cs_free = [None] * 2
    out_sem = [DmaSem(nc, "out%d" % i) for i in range(2)]
    st = dict(xt_n=0, xr_n=0, pt_n=0, chunk_n=0, ct_n=0, rot_n=0)
    PT_free = [None] * NPT
    free_tok = {}

    def fr(name):
        return free_tok.get(name)

    x_loaded = {}

    def issue_x_load(s, i):
        toks = []
        for tt in range(2):
            b = st["xt_n"] % NXB
            st["xt_n"] += 1
            SP.wait(xt_free[b])
            T = 2 * i + tt
            toks.append((b, xt_sem[b].issue(LZp.dma_start(out=xt[b][:], in_=x_d[s, T * 128:(T + 1) * 128, :]), SP, 524288)))
        x_loaded[(s, i)] = toks

    cs_loaded = {}

    def issue_cs_load(s, i):
        SP.wait(cs_free[0])
        tcs = cs_sem[0].issue(LZp.dma_start(out=cosb[0][:], in_=cos_d[:, i * BLK:(i + 1) * BLK]), SP, 131072)
        tcs = cs_sem[0].issue(LZp.dma_start(out=sinb[0][:], in_=sin_d[:, i * BLK:(i + 1) * BLK]), SP, 131072)
        cs_loaded[(s, i)] = tcs

    order = [(s, i) for s in range(NSEQ) for i in range(NB)]
    banks.free.remove(7)
    LZt, LZs, LZv, LZg, LZp = LazyEng(PE), LazyEng(ACT), LazyEng(DVE), LazyEng(POOL), LazyEng(SP)
    Sched.ops = []
    Sched.on = True

    class Cx:
        pass

    def emit_gate_bc(s):
        for half in range(2):
            bi_, bk, brel = banks.get()
            PE.wait(brel, t_gate_rows, t_pool_consts)
            tp = PE.done(LZt.matmul(bk[:, :], sel4[:, s, :], gate_rows[:, half * 512:(half + 1) * 512],
                                          start=True, stop=True))
            ACT.wait(tp, fr("gate_bc"))
            tg = ACT.done(LZs.copy(gate_bc[:, half * 512:(half + 1) * 512], bk[:, :]))
            banks.put(bi_, tg)
        free_tok["gate_bc_ready"] = tg

    def ef_A0(cx):
        s, i = cx.s, cx.i
        cx.xtoks = x_loaded.pop((s, i))
        cx.t_cs = cs_loaded.pop((s, i))
        xtoks = cx.xtoks
        if i == 0:
            POOL.wait([fr("xl%d" % c_) for c_ in range(4)])
            th = POOL.done(LZg.memset(xl[:, :, 0:3], 0.0))
            POOL.wait([fr("hc%d" % c_) for c_ in range(4)], [fr("rec%d" % c_) for c_ in range(2)])
            th2 = POOL.done(LZg.memset(hcarry[:], 0.0))
            free_tok["xl_halo_ready"] = th
            free_tok["hcarry_ready"] = th2
        t_xn = []
        for tt in range(2):
            b, tl = xtoks[tt]
            ACT.wait(tl, fr("junk"), fr("stat_in"))
            tq = ACT.done(LZs.activation(out=junk[:], in_=xt[b][:], func=AF.Square,
                                               accum_out=stat[:, tt:tt + 1]))
            free_tok["junk"] = tq
            t_xn.append(tq)
        ACT.wait(t_xn)
        tq = ACT.done(LZs.activation(out=stat[:, 2:4], in_=stat[:, 0:2], func=AF.Ln, scale=1.0 / D, bias=EPS))
        ACT.wait(tq)
        t_rstd = ACT.done(LZs.activation(out=stat[:, 4:6], in_=stat[:, 2:4], func=AF.Exp, scale=-0.5))
        t_xn2 = []
        for tt in range(2):
            b, tl = xtoks[tt]
            DVE.wait(t_rstd, tl)
            t_xn2.append(DVE.done(LZv.tensor_scalar(out=xt[b][:], in0=xt[b][:], scalar1=stat[:, 4 + tt:5 + tt],
                                                          scalar2=None, op0=ALU.mult)))
        free_tok["stat_in"] = t_xn2
        t_hT = []
        t_tr_all = []
        for fp in range(4):
            bi_, bk, brel = banks.get()
            PE.wait(brel, t_xn2)
            for f2 in range(2):
                fc = 2 * fp + f2
                for tt in range(2):
                    b, _ = xtoks[tt]
                    mm = LZt.transpose(bk[:, f2 * 256 + tt * 128:f2 * 256 + (tt + 1) * 128],
                                             xt[b][:, fc * 128:(fc + 1) * 128], ident[:])
            tp = PE.done(mm)
            t_tr_all.append(tp)
            evs = []
            for f2 in range(2):
                fc = 2 * fp + f2
                if fp % 2 == 0:
                    ACT.wait(tp, fr("hT"))
                    evs.append(ACT.done(LZs.activation(out=hT[:, fc, :], in_=bk[:, f2 * 256:(f2 + 1) * 256],
                                                             func=AF.Identity,
                                                             scale=gsT[:, fc, s:s + 1], bias=shT[:, fc, s:s + 1])))
                else:
                    DVE.wait(tp, fr("hT"))
                    evs.append(DVE.done(LZv.tensor_scalar(out=hT[:, fc, :], in0=bk[:, f2 * 256:(f2 + 1) * 256],
                                                                scalar1=gsT[:, fc, s:s + 1], scalar2=shT[:, fc, s:s + 1],
                                                                op0=ALU.mult, op1=ALU.add)))
            banks.put(bi_, evs)
            t_hT.extend(evs)
        for tt in range(2):
            xt_free[xtoks[tt][0]] = list(t_tr_all)
        cx.t_hT = t_hT
        cx.t_hT_rd = []
        cx.t_cs_rd = []
        if cx.oi + 1 < len(order):
            issue_x_load(*order[cx.oi + 1])

    def inproj_fm(col0, bk, half):
        for kc in range(8):
            mm_ = LZt.matmul(bk[:, half * 256:(half + 1) * 256], win_sb[:, kc, col0:col0 + 128], hT[:, kc, :],
                                   start=(kc == 0), stop=(kc == 7))
        return mm_

    def rope_tiles(cx, kind):
        s, i = cx.s, cx.i
        c0 = i * BLK
        colbase = 512 if kind == "k" else 0
        toks = []
        t_kmean = []
        for ht in range(4):
            bi_, bk, brel = banks.get()
            PE.wait(brel, cx.t_hT)
            tp = PE.done(inproj_fm(colbase + ht * 128, bk, 0))
            r = st["rot_n"] % 2
            st["rot_n"] += 1
            DVE.wait(tp, fr("kq32_%d" % r))
            tc = DVE.done(LZv.tensor_copy(kq32[r][:], bk[:, 0:256]))
            PE.wait(tc)
            tr = PE.done(LZt.matmul(bk[:, 256:512], rmat[:], kq32[r][:], start=True, stop=True))
            cx.t_hT_rd.append(tp)
            POOL.wait(tc, cx.t_cs, fr("rt1_%d" % r))
            tm1 = POOL.done(LZg.tensor_mul(rt1[r][:], kq32[r][:], cosb[0][:]))
            DVE.wait(tr, cx.t_cs, fr("rt2_%d" % r))
            tm2 = DVE.done(LZv.tensor_mul(rt2[r][:], bk[:, 256:512], sinb[0][:]))
            cx.t_cs_rd += [tm1, tm2]
            banks.put(bi_, tm2)
            free_tok["kq32_%d" % r] = [tm1, tr]
            DVE.wait(tm1, tm2)
            if kind == "k":
                DVE.wait(fr("krot"))
                t3 = DVE.done(LZv.tensor_add(krot[:], rt1[r][:], rt2[r][:]))
                free_tok["rt1_%d" % r] = t3
                free_tok["rt2_%d" % r] = t3
                ACT.wait(t3, fr("kT"))
                tk = ACT.done(LZs.copy(kT[:, ht, c0:c0 + BLK], krot[:]))
                toks.append(tk)
                DVE.wait(t3, fr("ksum"))
                t4 = DVE.done(LZv.tensor_reduce(out=ksum[:, ht:ht + 1], in_=krot[:], axis=AX.X, op=ALU.add))
                free_tok["krot"] = [tk, t4]
                DVE.wait(t4, fr("kmean"))
                t5 = DVE.done(LZv.tensor_scalar(out=kmean[:, ht, i:i + 1], in0=ksum[:, ht:ht + 1],
                                                      scalar1=1.0 / BLK, scalar2=None, op0=ALU.mult))
                free_tok["ksum"] = t5
                t_kmean.append(t5)
            else:
                DVE.wait(fr("qT"))
                t3 = DVE.done(LZv.tensor_add(qT[:, ht, :], rt1[r][:], rt2[r][:]))
                free_tok["rt1_%d" % r] = t3
                free_tok["rt2_%d" % r] = t3
                toks.append(t3)
        if kind == "k":
            cx.t_kT = toks
            cx.t_kmean = t_kmean
        else:
            cx.t_qT = toks

    def ef_K(cx):
        rope_tiles(cx, "k")

    def ef_V(cx):
        i = cx.i
        t_v = []
        for tt in range(2):
            T = 2 * i + tt
            bi_, bk, brel = banks.get()
            PE.wait(brel, cx.t_hT)
            for kc in range(8):
                mm = LZt.matmul(bk[:, :], hT[:, kc, tt * 128:(tt + 1) * 128], win_sb[:, kc, 1024:1536],
                                      start=(kc == 0), stop=(kc == 7))
            tp = PE.done(mm)
            cx.t_hT_rd.append(tp)
            DVE.wait(tp, fr("vaug"), fr("vaug_chain"))
            tv = DVE.done(LZv.tensor_copy(vaug[:, T, :, 0:HD], bk[:, :].rearrange("p (h d) -> p h d", h=NH)))
            banks.put(bi_, tv)
            free_tok["vaug_chain"] = tv
            t_v.append(tv)
        cx.t_v = t_v

    def ef_L(cx, ci):
        s, i = cx.s, cx.i
        p = ci % 2
        ylT = ylT2[cx.oi % 2]
        if ci == 0:
            cx.t_yp = []
        bi_, bk, brel = banks.get()
        PE.wait(brel, cx.t_hT)
        tpx = PE.done(inproj_fm(2048 + ci * 128, bk, 0))
        PE.wait(tpx, cx.t_hT)
        tpz = PE.done(inproj_fm(2560 + ci * 128, bk, 1))
        cx.t_hT_rd += [tpx, tpz]
        ACT.wait(tpx, tpz, fr("xl%d" % ci), fr("xl_halo_ready"))
        tx = ACT.done(LZs.copy(xl[:, ci, 3:3 + BLK], bk[:, 0:256]))
        ACT.wait(tpz, fr("uu%d" % p))
        tez = ACT.done(LZs.activation(out=uu[p][:], in_=bk[:, 256:512], func=AF.Exp, scale=-1.0))
        ACT.wait(tez)
        t8 = ACT.done(LZs.activation(out=uu[p][:], in_=uu[p][:], func=AF.Ln, bias=1.0))
        ACT.wait(t8)
        t8 = ACT.done(LZs.activation(out=uu[p][:], in_=uu[p][:], func=AF.Exp, scale=-1.0))
        DVE.wait(t8, tpz, fr("szl%d" % ci))
        t_szl = DVE.done(LZv.tensor_mul(szl[:, ci, :], bk[:, 256:512], uu[p][:]))
        banks.put(bi_, [tx, t_szl])
        free_tok["uu%d" % p] = t_szl
        POOL.wait(tx, fr("xc%d" % p), fr("xl%d" % ci), fr("xl_halo_ready"))
        t9 = POOL.done(LZg.tensor_scalar(out=xc[p][:], in0=xl[:, ci, 0:BLK],
                                               scalar1=vecs[:, V_CW + ci * 4:V_CW + ci * 4 + 1],
                                               scalar2=vecs[:, V_CB + ci:V_CB + ci + 1], op0=ALU.mult, op1=ALU.add))
        for w in range(1, 4):
            DVE.wait(t9)
            t9 = DVE.done(LZv.scalar_tensor_tensor(out=xc[p][:], in0=xl[:, ci, w:w + BLK],
                                                         scalar=vecs[:, V_CW + ci * 4 + w:V_CW + ci * 4 + w + 1],
                                                         in1=xc[p][:], op0=ALU.mult, op1=ALU.add))
        POOL.wait(t9, fr("xcb%d" % p))
        t10 = POOL.done(LZg.tensor_copy(xcb[p][:], xc[p][:]))
        POOL.wait(t9, tx)
        t_halo = POOL.done(LZg.tensor_copy(xl[:, ci, 0:3], xl[:, ci, BLK:BLK + 3]))
        free_tok["xl%d" % ci] = t_halo
        bi_, bk, brel = banks.get()
        PE.wait(brel, t10)
        LZt.matmul(bk[:, 0:256], wr_bd[:, ci, :], xcb[p][:], start=True, stop=True)
        tpg = PE.done(LZt.matmul(bk[:, 256:512], wi_bd[:, ci, :], xcb[p][:], start=True, stop=True))
        free_tok["xcb%d" % p] = tpg
        ACT.wait(tpg, fr("er%d" % p), fr("ei%d" % p))
        ter = ACT.done(LZs.activation(out=er[p], in_=bk[:, 0:256], func=AF.Exp, scale=-1.0,
                                            bias=nbr[:, ci:ci + 1]))
        ACT.wait(tpg, fr("ei%d" % p), fr("er%d" % p))
        tei = ACT.done(LZs.activation(out=ei[p], in_=bk[:, 256:512], func=AF.Exp, scale=-1.0,
                                            bias=nbi[:, ci:ci + 1]))
        banks.put(bi_, [ter, tei])
        ACT.wait(ter, tei)
        t11 = ACT.done(LZs.activation(out=eri[p][:, :, :], in_=eri[p][:, :, :], func=AF.Ln, bias=1.0))
        ACT.wait(t11)
        t11 = ACT.done(LZs.activation(out=eri[p][:, :, :], in_=eri[p][:, :, :], func=AF.Exp, scale=-1.0))
        t12 = t11
        ACT.wait(t11, fr("aa%d" % p))
        ta_ = ACT.done(LZs.activation(out=aa[p][:], in_=er[p], func=AF.Exp, scale=coef[:, ci:ci + 1]))
        POOL.wait(ta_, fr("a2_%d" % p))
        ta2 = POOL.done(LZg.tensor_mul(a2[p][:], aa[p][:], aa[p][:]))
        free_tok["er%d" % p] = ta_
        ACT.wait(ta2)
        ta2 = ACT.done(LZs.activation(out=a2[p][:], in_=a2[p][:], func=AF.Ln, scale=-1.0, bias=1.0))
        ACT.wait(ta2)
        tnrm = ACT.done(LZs.activation(out=a2[p][:], in_=a2[p][:], func=AF.Exp, scale=0.5))
        DVE.wait(t12, t9)
        t12 = DVE.done(LZv.tensor_mul(ei[p], ei[p], xc[p][:]))
        free_tok["xc%d" % p] = t12
        DVE.wait(t12, tnrm)
        t13 = DVE.done(LZv.tensor_mul(ei[p], ei[p], a2[p][:]))
        free_tok["a2_%d" % p] = t13
        DVE.wait(t13, ta_, fr("rec%d" % p), fr("hcarry_ready"), fr("hc%d" % ci))
        t14 = DVE.done(LZv.tensor_tensor_scan(out=rec[p][:], data0=aa[p][:], data1=ei[p],
                                                    initial=hcarry[:, ci:ci + 1], op0=ALU.mult, op1=ALU.add))
        free_tok["aa%d" % p] = t14
        free_tok["ei%d" % p] = t14
        DVE.wait(t14)
        t15 = DVE.done(LZv.tensor_copy(hcarry[:, ci:ci + 1], rec[p][:, BLK - 1:BLK]))
        free_tok["hc%d" % ci] = t15
        POOL.wait(t14, fr("sqb%d" % p))
        t16 = POOL.done(LZg.tensor_mul(sqb[p][:], rec[p][:], rec[p][:]))
        DVE.wait(t_szl, t14, fr("yp"), fr("attn"))
        t17 = DVE.done(LZv.scalar_tensor_tensor(out=yp[:, ci, :], in0=rec[p][:],
                                                      scalar=vecs[:, V_GL + ci:V_GL + ci + 1], in1=szl[:, ci, :],
                                                      op0=ALU.mult, op1=ALU.mult))
        free_tok["rec%d" % p] = [t16, t17, t15]
        free_tok["szl%d" % ci] = t17
        cx.t_yp.append(t17)
        if ci == 0:
            cx.sb_ = banks.get()
            PE.wait(cx.sb_[2])
        sbi, sbk, _ = cx.sb_
        PE.wait(t16, cx.t_stats_prev if ci > 0 else None)
        tps = PE.done(LZt.matmul(sbk[:, 0:256], onesm[:], sqb[p][:], start=(ci == 0), stop=(ci == 3)))
        cx.t_stats_prev = tps
        free_tok["sqb%d" % p] = tps
        if ci == 3:
            free_tok["xl_halo"] = t_halo
            free_tok["hcarry"] = t15
            ACT.wait(tps, fr("rstd_bc"))
            tl1 = ACT.done(LZs.activation(out=rstd_bc[:], in_=sbk[:, 0:256], func=AF.Ln, bias=EPS))
            banks.put(sbi, tl1)
            ACT.wait(tl1)
            tl2 = ACT.done(LZs.activation(out=rstd_bc[:], in_=rstd_bc[:], func=AF.Exp, scale=-0.5))
            t_ylT = []
            for c2 in range(4):
                E = POOL if c2 % 2 else DVE
                E.wait(tl2, cx.t_yp[c2], fr("ylT%d" % (cx.oi % 2)))
                fn = LZg.tensor_mul if c2 % 2 else LZv.tensor_mul
                t_ylT.append(E.done(fn(ylT[:, c2, :], yp[:, c2, :], rstd_bc[:])))
            free_tok["rstd_bc"] = t_ylT
            free_tok["yp"] = t_ylT
            cx.t_ylT = t_ylT

    def lf_Q(cx):
        rope_tiles(cx, "q")
        cs_free[0] = list(cx.t_cs_rd)
        if cx.oi + 1 < len(order):
            issue_cs_load(*order[cx.oi + 1])

    def lf_Z(cx):
        t_sza = []
        for tt in range(2):
            bi_, bk, brel = banks.get()
            PE.wait(brel, cx.t_hT)
            for kc in range(8):
                mm = LZt.matmul(bk[:, :], hT[:, kc, tt * 128:(tt + 1) * 128], win_sb[:, kc, 1536:2048],
                                      start=(kc == 0), stop=(kc == 7))
            tp = PE.done(mm)
            cx.t_hT_rd.append(tp)
            ACT.wait(tp, fr("ez%d" % tt))
            te = ACT.done(LZs.activation(out=ez[tt][:], in_=bk[:, :], func=AF.Exp, scale=-1.0))
            ACT.wait(te)
            t6 = ACT.done(LZs.activation(out=ez[tt][:], in_=ez[tt][:], func=AF.Ln, bias=1.0))
            ACT.wait(t6)
            t6 = ACT.done(LZs.activation(out=ez[tt][:], in_=ez[tt][:], func=AF.Exp, scale=-1.0))
            DVE.wait(t6, tp, fr("sza"))
            t7 = DVE.done(LZv.tensor_mul(sza[:, tt, :], bk[:, :], ez[tt][:]))
            free_tok["ez%d" % tt] = t7
            banks.put(bi_, t7)
            t_sza.append(t7)
        cx.t_sza = t_sza
        free_tok["hT"] = list(cx.t_hT_rd)

    def at_SEL(cx):
        i = cx.i
        cx.t_acc = []
        cx.t_PV_all = []
        cx.t_ST_all = []
        cx.t_tcm_all = []
        cx.t_sel = None
        if i < 4:
            return
        gb = [banks.get() for _ in range(2)]
        PE.wait(gb[0][2], gb[1][2], cx.t_qT, cx.t_kmean)
        for tt in range(2):
            for ht in range(4):
                for hh in range(2):
                    hb = hh * 64
                    col = (tt * 4 + ht) * 8
                    mm = LZt.matmul(gb[hh][1][:, col:col + i],
                                          qT[hb:hb + 64, ht, tt * 128:(tt + 1) * 128],
                                          kmean[hb:hb + 64, ht, 0:i], start=True, stop=True)
        tpg = PE.done(mm)
        g5 = g32[:, :, :, :].rearrange("p t (a b) j -> p t a b j", b=2)
        tg32s = []
        for hh in range(2):
            ACT.wait(tpg, fr("g32"))
            tg32 = ACT.done(LZs.copy(
                g5[:, :, :, hh, 0:i],
                gb[hh][1][:, 0:64].rearrange("p (t a j) -> p t a j", t=2, a=4)[:, :, :, 0:i]))
            banks.put(gb[hh][0], tg32)
            tg32s.append(tg32)
        t_sel = []
        for tt in range(2):
            gv = g32[:, tt, :, 0:i]
            DVE.wait(tg32s, fr("cmpb"))
            tc1 = DVE.done(LZv.tensor_tensor(out=cmpb[:, :, 0:i, 0:i],
                                                   in0=gv.unsqueeze(2).to_broadcast((128, NH, i, i)),
                                                   in1=gv.unsqueeze(3).to_broadcast((128, NH, i, i)),
                                                   op=ALU.is_gt))
            DVE.wait(tc1, fr("rank"))
            tc2 = DVE.done(LZv.tensor_reduce(out=rank[:, :, 0:i], in_=cmpb[:, :, 0:i, 0:i], axis=AX.X,
                                                   op=ALU.add))
            free_tok["cmpb"] = tc2
            DVE.wait(tc2, fr("selx"))
            tc3 = DVE.done(LZv.tensor_single_scalar(out=selx[:, tt, :, 0:i], in_=rank[:, :, 0:i], scalar=3.0,
                                                          op=ALU.is_lt))
            free_tok["rank"] = tc3
            t_sel.append(tc3)
        free_tok["g32"] = t_sel
        cx.t_sel = t_sel

    def at_P(cx, hp):
        i = cx.i
        steps = [i] + list(range(i))
        st_state = {}
        st_acc = {}
        accp = acc[:, :, 2 * hp:2 * hp + 2, :]

        def emit_st(j):
            own = (j == i)
            bx = [banks.get() for _ in range(2)]
            PE.wait(bx[0][2], bx[1][2], cx.t_qT, cx.t_kT)
            mm_ = None
            for ktl in range(2):
                kt = 2 * j + ktl
                qlo = 128 if (own and ktl == 1) else 0
                for hh in range(2):
                    hb = hh * 64
                    mm_ = LZt.matmul(bx[hh][1][:, ktl * 256 + qlo:(ktl + 1) * 256],
                                           kT[hb:hb + 64, hp, kt * 128:(kt + 1) * 128],
                                           qT[hb:hb + 64, hp, qlo:256], start=True, stop=True)
            tps_ = PE.done(mm_)
            cx.t_ST_all.append(tps_)
            res_ = []
            for hh in range(2):
                pb = st["pt_n"] % NPT
                st["pt_n"] += 1
                ACT.wait(tps_, PT_free[pb])
                if own:
                    te1_ = ACT.done(LZs.activation(out=PT[pb][:, 0, :], in_=bx[hh][1][:, 0:256], func=AF.Exp,
                                                   scale=0.125))
                    ACT.wait(tps_, PT_free[pb])
                    te_ = ACT.done(LZs.activation(out=PT[pb][:, 1, 128:256], in_=bx[hh][1][:, 384:512],
                                                        func=AF.Exp, scale=0.125))
                else:
                    te_ = ACT.done(LZs.activation(out=PT[pb][:, :, :].rearrange("p a q -> p (a q)"),
                                                        in_=bx[hh][1][:, :], func=AF.Exp, scale=0.125))
                banks.put(bx[hh][0], [te_, te1_] if own else te_)
                tok = te_
                if own:
                    POOL.wait(te1_, t_pool_consts)
                    tm1_ = POOL.done(LZg.tensor_mul(PT[pb][:, 0, 0:128], PT[pb][:, 0, 0:128], tri[:]))
                    POOL.wait(te_, t_pool_consts)
                    tm2_ = POOL.done(LZg.tensor_mul(PT[pb][:, 1, 128:256], PT[pb][:, 1, 128:256], tri[:]))
                    tok = [tm1_, tm2_, te1_, te_]
                res_.append((pb, tok))
            st_state[j] = res_

        def emit_pv(j):
            own = (j == i)
            (pbA, tokA), (pbB, tokB) = st_state.pop(j)
            zi, zk, zrel = banks.get()
            PE.wait(tokA, tokB, cx.t_v, zrel)
            mm_ = None
            for tt in range(2):
                for hh in range(2):
                    h = 2 * hp + hh
                    pb = (pbA, pbB)[hh]
                    g = tt * 2 + hh
                    dst = zk[:, g * 65:(g + 1) * 65]
                    ktls = [0] if (own and tt == 0) else [0, 1]
                    for n_, ktl in enumerate(ktls):
                        mm_ = LZt.matmul(dst, PT[pb][:, ktl, tt * 128:(tt + 1) * 128],
                                               vaug[:, 2 * j + ktl, h, :],
                                               start=(n_ == 0), stop=(n_ == len(ktls) - 1))
            tpv_ = PE.done(mm_)
            cx.t_PV_all.append(tpv_)
            PT_free[pbA] = tpv_
            PT_free[pbB] = tpv_
            zv = zk[:, 0:4 * 65].rearrange("p (t h d) -> p t h d", t=2, h=2)
            if own:
                ACT.wait(tpv_, fr("acc"))
                ta_ = ACT.done(LZs.copy(accp, zv))
                banks.put(zi, ta_)
            elif i <= 3:
                DVE.wait(tpv_, st_acc["tok"])
                ta_ = DVE.done(LZv.tensor_add(accp, accp, zv))
                banks.put(zi, ta_)
            else:
                cb2 = st["ct_n"] % 2
                st["ct_n"] += 1
                DVE.wait(tpv_, cx.t_sel, fr("ctmp%d" % cb2))
                tcm = DVE.done(LZv.tensor_tensor(
                    out=ctmp[cb2][:, :, :, :], in0=zv,
                    in1=selx[:, :, 2 * hp:2 * hp + 2, j].unsqueeze(3).to_broadcast((128, 2, 2, 65)),
                    op=ALU.mult))
                banks.put(zi, tcm)
                cx.t_tcm_all.append(tcm)
                POOL.wait(tcm, st_acc["tok"])
                ta_ = POOL.done(LZg.tensor_add(accp, accp, ctmp[cb2][:, :, :, :]))
                free_tok["ctmp%d" % cb2] = ta_
            st_acc["tok"] = ta_
            return tpv_

        tpv = None
        emit_st(steps[0])
        if len(steps) > 1:
            emit_st(steps[1])
        for n_ in range(len(steps)):
            tpv = emit_pv(steps[n_])
            if n_ + 2 < len(steps):
                emit_st(steps[n_ + 2])
        cx.t_acc.append(st_acc["tok"])
        if hp == 3:
            free_tok["qT"] = list(cx.t_ST_all)
            if i >= 4:
                free_tok["selx"] = list(cx.t_tcm_all)

    def bk_N(cx):
        t_ya = []
        for tt in range(2):
            DVE.wait(cx.t_acc, fr("rden"))
            tn1 = DVE.done(LZv.reciprocal(rden[:, tt, :], acc[:, tt, :, HD]))
            DVE.wait(tn1, fr("attn"), fr("yp"))
            tn2 = DVE.done(LZv.tensor_mul(attn[:, tt, :].rearrange("p (h d) -> p h d", h=NH),
                                                acc[:, tt, :, 0:HD],
                                                rden[:, tt, :].unsqueeze(2).to_broadcast((128, NH, HD))))
            ACT.wait(tn2, fr("junk"), fr("stat_a"))
            tn3 = ACT.done(LZs.activation(out=junk[:, 0:512], in_=attn[:, tt, :], func=AF.Square,
                                                accum_out=stat[:, 6 + tt:7 + tt]))
            free_tok["junk"] = tn3
            ACT.wait(tn3)
            tn4 = ACT.done(LZs.activation(out=stat[:, 8 + tt:9 + tt], in_=stat[:, 6 + tt:7 + tt], func=AF.Ln,
                                                scale=1.0 / 512.0, bias=EPS))
            ACT.wait(tn4)
            tn5 = ACT.done(LZs.activation(out=stat[:, 10 + tt:11 + tt], in_=stat[:, 8 + tt:9 + tt], func=AF.Exp,
                                                scale=-0.5))
            DVE.wait(tn5, cx.t_sza[tt])
            tn6 = DVE.done(LZv.scalar_tensor_tensor(out=attn[:, tt, :], in0=attn[:, tt, :],
                                                          scalar=stat[:, 10 + tt:11 + tt], in1=sza[:, tt, :],
                                                          op0=ALU.mult, op1=ALU.mult))
            t_ya.append(tn6)
        free_tok["acc"] = t_ya
        free_tok["rden"] = t_ya
        free_tok["sza"] = t_ya
        free_tok["stat_a"] = t_ya
        t_yaT = []
        t_tr2_all = []
        for mp in range(2):
            bi_, bk, brel = banks.get()
            PE.wait(brel, t_ya, t_pool_consts)
            for m2 in range(2):
                m = 2 * mp + m2
                for tt in range(2):
                    mm = LZt.transpose(bk[:, m2 * 256 + tt * 128:m2 * 256 + (tt + 1) * 128],
                                             attn[:, tt, m * 128:(m + 1) * 128], ident[:])
            tp = PE.done(mm)
            evs = []
            for m2 in range(2):
                m = 2 * mp + m2
                ACT.wait(tp, fr("yaT"))
                evs.append(ACT.done(LZs.activation(out=yaT[:, m, :], in_=bk[:, m2 * 256:(m2 + 1) * 256],
                                                         func=AF.Identity, scale=vecs[:, V_GA + m:V_GA + m + 1])))
            banks.put(bi_, evs)
            t_yaT.extend(evs)
            t_tr2_all.append(tp)
        free_tok["attn"] = t_tr2_all
        cx.t_yaT = t_yaT

    def bk_O(cx):
        s, i = cx.s, cx.i
        ylT = ylT2[cx.oi % 2]
        t_oproj = []
        t_to1 = []
        pre = []
        for tt in range(2):
            T = 2 * i + tt
            rb = st["xr_n"] % 2
            st["xr_n"] += 1
            SP.wait(xr_free[rb])
            pre.append((rb, xr_sem[rb].issue(LZp.dma_start(out=xr[rb][:], in_=x_d[s, T * 128:(T + 1) * 128, :]),
                                             SP, 524288)))
        for tt in range(2):
            T = 2 * i + tt
            rb, t_xr = pre[tt]
            t_res = []
            for half in range(2):
                bi_, bk, brel = banks.get()
                PE.wait(brel, cx.t_yaT, cx.t_ylT)
                for kc in range(8):
                    lhs = yaT[:, kc, tt * 128:(tt + 1) * 128] if kc < 4 else ylT[:, kc - 4, tt * 128:(tt + 1) * 128]
                    mm = LZt.matmul(bk[:, :], lhs, wout_sb[:, kc, half * 512:(half + 1) * 512],
                                          start=(kc == 0), stop=(kc == 7))
                tp = PE.done(mm)
                t_oproj.append(tp)
                ob2 = half
                DVE.wait(tp, fr("ez%d" % ob2), fr("gate_bc_ready"))
                to1 = DVE.done(LZv.tensor_mul(otmp[ob2][:], bk[:, :], gate_bc[:, half * 512:(half + 1) * 512]))
                banks.put(bi_, to1)
                t_to1.append(to1)
                DVE.wait(to1, t_xr)
                to2 = DVE.done(LZv.tensor_add(xr[rb][:, half * 512:(half + 1) * 512],
                                              xr[rb][:, half * 512:(half + 1) * 512], otmp[ob2][:]))
                free_tok["ez%d" % ob2] = to2
                t_res.append(to2)
            ACT.wait(t_res, fr("junk"), fr("stat_f"))
            tf1 = ACT.done(LZs.activation(out=junk[:], in_=xr[rb][:], func=AF.Square,
                                                accum_out=stat[:, 12:13]))
            free_tok["junk"] = tf1
            ACT.wait(tf1)
            tf2 = ACT.done(LZs.activation(out=stat[:, 13:14], in_=stat[:, 12:13], func=AF.Ln, scale=1.0 / D, bias=EPS))
            ACT.wait(tf2)
            tf3 = ACT.done(LZs.activation(out=stat[:, 14:15], in_=stat[:, 13:14], func=AF.Exp, scale=-0.5))
            DVE.wait(tf3, t_res)
            tf4 = DVE.done(LZv.scalar_tensor_tensor(out=xr[rb][:], in0=xr[rb][:], scalar=stat[:, 14:15],
                                                          in1=fgain_bc[:], op0=ALU.mult, op1=ALU.mult))
            free_tok["stat_f"] = tf4
            SP.wait(tf4)
            t_out = out_sem[rb].issue(LZp.dma_start(out=out_d[s, T * 128:(T + 1) * 128, :], in_=xr[rb][:]), SP, 524288)
            xr_free[rb] = t_out
        free_tok["yaT"] = list(t_oproj)
        free_tok["ylT%d" % (cx.oi % 2)] = list(t_oproj)
        if i == NB - 1:
            free_tok["gate_bc"] = [tf4] + t_to1
            if s + 1 < NSEQ:
                emit_gate_bc(s + 1)

    def make_cx(oi):
        cx = Cx()
        cx.oi = oi
        cx.s, cx.i = order[oi]
        return cx

    def ef_pieces(cx):
        return [lambda: ef_A0(cx), lambda: ef_K(cx), lambda: ef_V(cx), lambda: ef_L(cx, 0), lambda: ef_L(cx, 1),
                lambda: ef_L(cx, 2), lambda: ef_L(cx, 3)]

    def at_pieces(cx):
        return [lambda: at_SEL(cx), lambda: at_P(cx, 0), lambda: at_P(cx, 1), lambda: at_P(cx, 2), lambda: at_P(cx, 3)]

    emit_gate_bc(0)
    issue_x_load(*order[0])
    issue_cs_load(*order[0])
    cur = make_cx(0)
    for f in ef_pieces(cur):
        f()
    lf_Q(cur)
    lf_Z(cur)
    for oi in range(len(order)):
        nxt = make_cx(oi + 1) if oi + 1 < len(order) else None
        A = at_pieces(cur)
        if nxt is None:
            for f in A:
                f()
        elif nxt.i == 0:
            for f in A:
                f()
            free_tok["kT"] = list(cur.t_ST_all)
            free_tok["vaug"] = list(cur.t_PV_all)
            free_tok["kmean"] = list(cur.t_ST_all) + list(cur.t_PV_all)
            for f in ef_pieces(nxt):
                f()
        else:
            E = ef_pieces(nxt)
            for f in (E[0], A[0], A[1], E[1], A[2], E[2], E[3], A[3], E[4], E[5], A[4], E[6]):
                f()
        bk_N(cur)
        if nxt is not None:
            lf_Q(nxt)
            lf_Z(nxt)
        bk_O(cur)
        cur = nxt

    for E_ in (PE, ACT, DVE, POOL, SP):
        assert not E_.p_calls and not E_.p_waits, E_.name
    scratch_bank = banks.t[7]

    def _pe_fill():
        nc.tensor.matmul(scratch_bank[:, :], win_sb[:, 0, 0:128], win_sb[:, 1, 0:512], start=True, stop=True)

    span_est = run_schedule((PE, ACT, DVE, POOL, SP), pe_fill=_pe_fill if PE_FILL else None)
    nc.sched_span_us = span_est
    for o_ in out_sem:
        SP.wait((o_.sem, o_.n, o_.key))
    return nc


_CACHE = {}


def _consts(S):
    pos = np.arange(S, dtype=np.float32)
    inv_freq = (np.float32(10000.0) ** (-np.arange(0, HD, 2, dtype=np.float32) / np.float32(HD))).astype(np.float32)
    ang = (pos[:, None] * inv_freq[None, :]).astype(np.float32)
    cos = np.cos(ang).astype(np.float32).T
    sin = np.sin(ang).astype(np.float32).T
    cosT = np.tile(cos, (4, 1))
    sinS = np.concatenate([-sin, sin, -sin, sin], axis=0)
    ident = np.eye(128, dtype=np.float32)
    rmat = np.zeros((128, 128), np.float32)
    for d in range(128):
        partner = (d // 64) * 64 + ((d % 64) + 32) % 64
        rmat[partner, d] = 1.0
    tri = np.triu(np.ones((128, 128), np.float32))
    return dict(cosT=np.ascontiguousarray(cosT), sinS=np.ascontiguousarray(sinS), ident=ident, rmat=rmat, tri=tri)


def _sel4(nseq):
    m = np.zeros((nseq, nseq, 128), np.float32)
    for b in range(nseq):
        m[b, b, :] = 1.0
    return m.reshape(nseq, nseq * 128)


def _fm(v, nchunk):
    return np.ascontiguousarray(np.asarray(v, np.float32).reshape(nchunk, 128).T)


def make_in_maps(inputs, n_cores, nseq, S):
    x = np.asarray(inputs["x"], np.float32)
    c = np.asarray(inputs["c"], np.float32)
    w_mod = np.ascontiguousarray(np.asarray(inputs["w_mod"], np.float32)[0])
    b_mod = np.asarray(inputs["b_mod"], np.float32)[0]
    vec = np.zeros((128, NVEC), np.float32)
    vec[:, 0:8] = _fm(inputs["norm_gain"][0], 8)
    vec[:, 8:16] = _fm(b_mod[0:D], 8)
    vec[:, 16:24] = _fm(b_mod[D:2 * D], 8)
    cw = np.asarray(inputs["conv_w"], np.float32)[0]
    vec[:, 24:40] = np.ascontiguousarray(cw.reshape(4, 4, 128).transpose(2, 1, 0)).reshape(128, 16)
    vec[:, 40:44] = _fm(inputs["conv_b"][0], 4)
    vec[:, 44:48] = _fm(inputs["lru_lambda"][0], 4)
    vec[:, 48:52] = _fm(np.asarray(inputs["b_rgate"], np.float32)[0].reshape(-1), 4)
    vec[:, 52:56] = _fm(np.asarray(inputs["b_igate"], np.float32)[0].reshape(-1), 4)
    vec[:, 56:60] = _fm(inputs["lru_out_gain"][0], 4)
    vec[:, 60:64] = _fm(inputs["attn_out_gain"][0], 4)
    common = dict(
        w_mod=w_mod,
        bgate=np.ascontiguousarray(b_mod[None, 2 * D:3 * D]),
        w_in=np.ascontiguousarray(np.asarray(inputs["w_in"], np.float32)[0]),
        w_out=np.ascontiguousarray(np.asarray(inputs["w_out"], np.float32)[0]),
        w_r=np.ascontiguousarray(np.asarray(inputs["w_rgate"], np.float32)[0]),
        w_i=np.ascontiguousarray(np.asarray(inputs["w_igate"], np.float32)[0]),
        vecs=vec,
        fgain=np.ascontiguousarray(np.asarray(inputs["final_gain"], np.float32)[None, :]),
    )
    common.update(_consts(S))
    common["sel4"] = _sel4(nseq)
    maps = []
    for k in range(n_cores):
        xs = np.ascontiguousarray(x[k * nseq:(k + 1) * nseq, :S])
        cs = c[k * nseq:(k + 1) * nseq]
        cT = np.ascontiguousarray(cs.reshape(nseq, 8, 128).transpose(2, 1, 0))
        m = dict(common)
        m["x"] = xs
        m["cT"] = cT
        maps.append(m)
    return maps


def kernel(**inputs):
    key = (NSEQ_FULL, S_FULL)
    if key not in _CACHE:
        _CACHE[key] = build_program(NSEQ_FULL, S_FULL)
    nc = _CACHE[key]
    maps = make_in_maps(inputs, N_CORES, NSEQ_FULL, S_FULL)
    res = run_bass_kernel_spmd(nc, maps, core_ids=list(range(N_CORES)))
    out = np.concatenate([np.asarray(r["out"], np.float32) for r in res.results], axis=0)
    return out
```
